# Optimizing a Trainium2 kernel written in Bass

```python
import math
import jax, jax.numpy as jnp
from jax import lax
import numpy as np

D_MODEL = 1024
BATCH = 2
SEQ = 8192
DEPTH = 4

HEAD_DIM = 64
MIX_WIDTH = D_MODEL
A_WIDTH = MIX_WIDTH // 4
A_HEADS = A_WIDTH // HEAD_DIM
CHUNK = 128
B_WIDTH = MIX_WIDTH // 2
B_Q_HEADS = B_WIDTH // HEAD_DIM
B_KV_HEADS = 2
B_GROUP = B_Q_HEADS // B_KV_HEADS
WINDOW = 128
ROPE_THETA = 10000.0
C_WIDTH = MIX_WIDTH // 4
C_GROUP = 16
C_GROUPS = C_WIDTH // C_GROUP
C_STATE = 64
DT_MIN = 0.001
DT_MAX = 0.1
IN_A = 2 * A_WIDTH
IN_Q = B_WIDTH
IN_KV = B_KV_HEADS * HEAD_DIM
IN_C = C_WIDTH
IN_COLS = IN_A + IN_Q + 2 * IN_KV + IN_C
D_FF = 4 * D_MODEL
PLE_DIM = 256
EPS = 1e-6

kernel_name = "hybrid_gmlp_swa_s5_trunk"


def rmsnorm(x, g):
    xf = x.astype(jnp.float32)
    y = xf * lax.rsqrt(jnp.mean(xf * xf, axis=-1, keepdims=True) + EPS)
    return (y * g.astype(jnp.float32)).astype(x.dtype)


def layernorm(x, g, b):
    xf = x.astype(jnp.float32)
    mu = jnp.mean(xf, axis=-1, keepdims=True)
    xc = xf - mu
    y = xc * lax.rsqrt(jnp.mean(xc * xc, axis=-1, keepdims=True) + EPS)
    return (y * g.astype(jnp.float32) + b.astype(jnp.float32)).astype(x.dtype)


def rope_tables(positions):
    inv = 1.0 / (ROPE_THETA ** (jnp.arange(0, HEAD_DIM, 2, dtype=jnp.float32) / HEAD_DIM))
    ang = positions.astype(jnp.float32)[..., None] * inv
    return jnp.cos(ang), jnp.sin(ang)


def apply_rope(x, cos, sin):
    xf = x.astype(jnp.float32)
    x1, x2 = jnp.split(xf, 2, axis=-1)
    c = cos[:, :, None, :]
    s = sin[:, :, None, :]
    return jnp.concatenate([x1 * c - x2 * s, x2 * c + x1 * s], axis=-1).astype(x.dtype)


def chunk_gmlp(z, ln_g, ln_b, w_s, b_s):
    bsz, L, _ = z.shape
    z = jax.nn.gelu(z).reshape(bsz, L // CHUNK, CHUNK, A_HEADS, 2 * HEAD_DIM)
    u, v = jnp.split(z, 2, axis=-1)
    v = layernorm(v, ln_g, ln_b)
    causal = jnp.tril(jnp.ones((CHUNK, CHUNK), dtype=bool))
    w = jnp.where(causal, w_s, 0.0).astype(v.dtype)
    sv = jnp.einsum('hts,bnshd->bnthd', w, v) + b_s.T[None, None, :, :, None].astype(v.dtype)
    return (u * sv).reshape(bsz, L, A_WIDTH)


def swa_sink_attention(q, k, v, sinks):
    bsz, L = q.shape[:2]
    nb = L // WINDOW
    qb = q.reshape(bsz, nb, WINDOW, B_KV_HEADS, B_GROUP, HEAD_DIM)

    def band(t):
        t = t.reshape(bsz, nb, WINDOW, B_KV_HEADS, HEAD_DIM)
        prev = jnp.pad(t[:, :-1], ((0, 0), (1, 0), (0, 0), (0, 0), (0, 0)))
        return jnp.concatenate([prev, t], axis=2)

    kb, vb = band(k), band(v)
    s = jnp.einsum('bnqhgd,bnkhd->bnhgqk', qb, kb,
                   preferred_element_type=jnp.float32) * (HEAD_DIM ** -0.5)
    qi = jnp.arange(WINDOW)[:, None] + WINDOW
    kj = jnp.arange(2 * WINDOW)[None, :]
    diff = qi - kj
    band_ok = (diff >= 0) & (diff < WINDOW)
    not_first = jnp.arange(nb)[:, None, None] > 0
    mask = band_ok[None] & (not_first | (kj >= WINDOW)[None])
    s = jnp.where(mask[None, :, None, None], s, -jnp.inf)
    sink = sinks.astype(jnp.float32).reshape(B_KV_HEADS, B_GROUP)[None, None, :, :, None, None]
    m = jnp.maximum(jnp.max(s, axis=-1, keepdims=True), sink)
    pr = jnp.exp(s - m)
    denom = jnp.sum(pr, axis=-1, keepdims=True) + jnp.exp(sink - m)
    o = jnp.einsum('bnhgqk,bnkhd->bnqhgd', (pr / denom).astype(v.dtype), vb)
    return o.reshape(bsz, L, B_WIDTH)


def s5_ssm(u, a_re, a_im, log_dt, b_re, b_im, c_re, c_im, d_skip, glu_w1, glu_w2):
    bsz, L, _ = u.shape
    uf = u.astype(jnp.float32).reshape(bsz, L, C_GROUPS, C_GROUP)
    lam = lax.complex(a_re.astype(jnp.float32), a_im.astype(jnp.float32))
    dt = jnp.exp(log_dt.astype(jnp.float32))[:, None]
    lam_bar = jnp.exp(lam * dt)
    bmat = lax.complex(b_re.astype(jnp.float32), b_im.astype(jnp.float32))
    b_bar = ((lam_bar - 1.0) / lam)[..., None] * bmat
    bu = jnp.einsum('gph,blgh->blgp', b_bar, uf.astype(jnp.complex64))
    a_seq = jnp.broadcast_to(lam_bar, bu.shape)

    def combine(e1, e2):
        a1, x1 = e1
        a2, x2 = e2
        return a1 * a2, a2 * x1 + x2

    _, states = lax.associative_scan(combine, (a_seq, bu), axis=1)
    cmat = lax.complex(c_re.astype(jnp.float32), c_im.astype(jnp.float32))
    y = jnp.real(jnp.einsum('ghp,blgp->blgh', cmat, states)) + d_skip.astype(jnp.float32) * uf
    y = jax.nn.gelu(y.reshape(bsz, L, C_WIDTH)).astype(u.dtype)
    return (y @ glu_w1) * jax.nn.sigmoid(y @ glu_w2)


def hybrid_layer(h, p_i, cos, sin, attn_norm_g, w_in, gmlp_ln_g, gmlp_ln_b, gmlp_ws, gmlp_bs,
                 q_norm_g, k_norm_g, sinks, ssm_a_re, ssm_a_im, ssm_log_dt, ssm_b_re, ssm_b_im,
                 ssm_c_re, ssm_c_im, ssm_d, glu_w1, glu_w2, mix_out_g, w_out,
                 mlp_norm_g, w_ff1, w_ff2, ple_norm_g, w_ple_gate, w_ple_proj):
    bsz, L, _ = h.shape
    xn = rmsnorm(h, attn_norm_g)
    z = xn @ w_in
    za, zq, zk, zv, zc = jnp.split(
        z, [IN_A, IN_A + IN_Q, IN_A + IN_Q + IN_KV, IN_A + IN_Q + 2 * IN_KV], axis=-1)
    ya = chunk_gmlp(za, gmlp_ln_g, gmlp_ln_b, gmlp_ws, gmlp_bs)
    q = zq.reshape(bsz, L, B_Q_HEADS, HEAD_DIM)
    k = zk.reshape(bsz, L, B_KV_HEADS, HEAD_DIM)
    v = zv.reshape(bsz, L, B_KV_HEADS, HEAD_DIM)
    q = apply_rope(rmsnorm(q, q_norm_g), cos, sin)
    k = apply_rope(rmsnorm(k, k_norm_g), cos, sin)
    yb = swa_sink_attention(q, k, v, sinks)
    yc = s5_ssm(zc, ssm_a_re, ssm_a_im, ssm_log_dt, ssm_b_re, ssm_b_im,
                ssm_c_re, ssm_c_im, ssm_d, glu_w1, glu_w2)
    y = jnp.concatenate([
        rmsnorm(ya, mix_out_g[:A_WIDTH]),
        rmsnorm(yb, mix_out_g[A_WIDTH:A_WIDTH + B_WIDTH]),
        rmsnorm(yc, mix_out_g[A_WIDTH + B_WIDTH:]),
    ], axis=-1)
    h = h + y @ w_out
    hn = rmsnorm(h, mlp_norm_g)
    h = h + jnp.square(jax.nn.relu(hn @ w_ff1)) @ w_ff2
    gate = jax.nn.sigmoid(rmsnorm(h, ple_norm_g) @ w_ple_gate)
    return h + gate * (p_i @ w_ple_proj)


def setup_inputs(seed: int = 0) -> dict:
    key = jax.random.key(seed)
    ks = iter(jax.random.split(key, 40))
    f32 = jnp.float32

    def nrm(shape, scale):
        return jax.random.normal(next(ks), shape, f32) * scale

    def gain(shape):
        return 1.0 + nrm(shape, 0.02)

    x = nrm((BATCH, SEQ, D_MODEL), 1.0)
    p = nrm((DEPTH, BATCH, SEQ, PLE_DIM), 1.0)
    offsets = jax.random.randint(next(ks), (BATCH, 1), 0, 1024, dtype=jnp.int32)
    positions = offsets + jnp.arange(SEQ, dtype=jnp.int32)[None, :]

    n_idx = jnp.arange(C_STATE, dtype=f32)
    ssm_a_re = -0.5 + nrm((DEPTH, C_GROUPS, C_STATE), 0.01)
    ssm_a_im = math.pi * n_idx[None, None, :] + nrm((DEPTH, C_GROUPS, C_STATE), 0.01)
    ssm_log_dt = jax.random.uniform(next(ks), (DEPTH, C_GROUPS), f32,
                                    math.log(DT_MIN), math.log(DT_MAX))
    b_scale = (2.0 * C_GROUP) ** -0.5
    c_scale = (2.0 * C_STATE) ** -0.5

    return {
        "x": x,
        "p": p,
        "positions": positions,
        "attn_norm_g": gain((DEPTH, D_MODEL)),
        "w_in": nrm((DEPTH, D_MODEL, IN_COLS), D_MODEL ** -0.5),
        "gmlp_ln_g": gain((DEPTH, A_HEADS, HEAD_DIM)),
        "gmlp_ln_b": nrm((DEPTH, A_HEADS, HEAD_DIM), 0.02),
        "gmlp_ws": nrm((DEPTH, A_HEADS, CHUNK, CHUNK), 0.5 * CHUNK ** -0.5),
        "gmlp_bs": gain((DEPTH, A_HEADS, CHUNK)),
        "q_norm_g": gain((DEPTH, HEAD_DIM)),
        "k_norm_g": gain((DEPTH, HEAD_DIM)),
        "sinks": nrm((DEPTH, B_Q_HEADS), 0.5),
        "ssm_a_re": ssm_a_re,
        "ssm_a_im": ssm_a_im,
        "ssm_log_dt": ssm_log_dt,
        "ssm_b_re": nrm((DEPTH, C_GROUPS, C_STATE, C_GROUP), b_scale),
        "ssm_b_im": nrm((DEPTH, C_GROUPS, C_STATE, C_GROUP), b_scale),
        "ssm_c_re": nrm((DEPTH, C_GROUPS, C_GROUP, C_STATE), c_scale),
        "ssm_c_im": nrm((DEPTH, C_GROUPS, C_GROUP, C_STATE), c_scale),
        "ssm_d": nrm((DEPTH, C_GROUPS, C_GROUP), 0.5),
        "glu_w1": nrm((DEPTH, C_WIDTH, C_WIDTH), C_WIDTH ** -0.5),
        "glu_w2": nrm((DEPTH, C_WIDTH, C_WIDTH), C_WIDTH ** -0.5),
        "mix_out_g": gain((DEPTH, MIX_WIDTH)),
        "w_out": nrm((DEPTH, MIX_WIDTH, D_MODEL), MIX_WIDTH ** -0.5),
        "mlp_norm_g": gain((DEPTH, D_MODEL)),
        "w_ff1": nrm((DEPTH, D_MODEL, D_FF), D_MODEL ** -0.5),
        "w_ff2": nrm((DEPTH, D_FF, D_MODEL), D_FF ** -0.5),
        "ple_norm_g": gain((DEPTH, D_MODEL)),
        "w_ple_gate": nrm((DEPTH, D_MODEL, D_MODEL), D_MODEL ** -0.5),
        "w_ple_proj": nrm((DEPTH, PLE_DIM, D_MODEL), 0.5 * PLE_DIM ** -0.5),
    }


def reference(x, p, positions, attn_norm_g, w_in, gmlp_ln_g, gmlp_ln_b, gmlp_ws, gmlp_bs,
              q_norm_g, k_norm_g, sinks, ssm_a_re, ssm_a_im, ssm_log_dt, ssm_b_re, ssm_b_im,
              ssm_c_re, ssm_c_im, ssm_d, glu_w1, glu_w2, mix_out_g, w_out,
              mlp_norm_g, w_ff1, w_ff2, ple_norm_g, w_ple_gate, w_ple_proj):
    cos, sin = rope_tables(positions)
    h = x
    for i in range(DEPTH):
        h = hybrid_layer(h, p[i], cos, sin, attn_norm_g[i], w_in[i], gmlp_ln_g[i], gmlp_ln_b[i],
                         gmlp_ws[i], gmlp_bs[i], q_norm_g[i], k_norm_g[i], sinks[i],
                         ssm_a_re[i], ssm_a_im[i], ssm_log_dt[i], ssm_b_re[i], ssm_b_im[i],
                         ssm_c_re[i], ssm_c_im[i], ssm_d[i], glu_w1[i], glu_w2[i],
                         mix_out_g[i], w_out[i], mlp_norm_g[i], w_ff1[i], w_ff2[i],
                         ple_norm_g[i], w_ple_gate[i], w_ple_proj[i])
    return h
```

```python
import math
import numpy as np
from contextlib import ExitStack
import concourse.bass as bass
import concourse.mybir as mybir
from concourse.bass_utils import run_bass_kernel_spmd

F32 = mybir.dt.float32
BF16 = mybir.dt.bfloat16
I32 = mybir.dt.int32
ALU = mybir.AluOpType
AF = mybir.ActivationFunctionType
AX = mybir.AxisListType
DSZ = {F32: 4, BF16: 2, I32: 4}
ENGS = ['pe', 'dve', 'act', 'pool', 'sp']

TOK = 2048
NT = 4
EPS = 1e-6
TWO_PI = 2.0 * math.pi


class View:
    __slots__ = ('ap', 'rect')

    def __init__(self, ap, rect):
        self.ap = ap
        self.rect = rect

    def f(self, fn):
        return View(fn(self.ap), self.rect)


class Buf:
    def __init__(self, name, t, ncols, dtype, byte_off=0, tdtype=None):
        self.name = name
        self.t = t
        self.ncols = ncols
        self.dtype = dtype
        self.esz = DSZ[dtype]
        self.byte_off = byte_off
        self.tdtype = tdtype if tdtype is not None else dtype

    def v(self, p0=0, p1=128, c0=0, c1=None):
        if c1 is None:
            c1 = self.ncols
        assert 0 <= p0 < p1 <= 128 and 0 <= c0 < c1 <= self.ncols, (self.name, p0, p1, c0, c1)
        b0 = self.byte_off + c0 * self.esz
        b1 = self.byte_off + c1 * self.esz
        tsz = DSZ[self.tdtype]
        ap = self.t[p0:p1, b0 // tsz:b1 // tsz]
        if self.tdtype != self.dtype:
            ap = ap.bitcast(self.dtype)
        return View(ap, (self.name, p0, p1, b0, b1))

    def sub(self, c0, ncols, dtype=None):
        dtype = dtype or self.dtype
        return Buf(self.name, self.t, ncols, dtype, self.byte_off + c0 * self.esz, self.tdtype)


def dram(ap, name):
    return View(ap, (name, 0, 128, 0, 1 << 40))


class Op:
    __slots__ = ('eng', 'fn', 'waits', 'signal', 'token', 'seq', 'dma', 'kind')


class Prog:
    def __init__(self, nc, es):
        self.nc = nc
        self.es = es
        self.ops = {e: [] for e in ENGS}
        self.acc = {}
        self.waited = {e: {} for e in ENGS}
        self.dma_cnt = {}
        self.dma_sems = {}
        self.eng_sems = {}

    def sbuf(self, name, ncols, dtype):
        t = self.es.enter_context(self.nc.sbuf_tensor(name, [128, ncols], dtype))
        return Buf(name, t, ncols, dtype)

    def psum_bank(self, name):
        t = self.es.enter_context(self.nc.psum_tensor(name, [128, 512], F32))
        return Buf(name, t, 512, F32)

    def _deps(self, reads, writes):
        deps = []
        for r in reads:
            name, p0, p1, f0, f1 = r.rect
            for (q0, q1, g0, g1, kind, op) in self.acc.get(name, ()):
                if kind == 'w' and p0 < q1 and q0 < p1 and f0 < g1 and g0 < f1:
                    deps.append(op)
        for w in writes:
            name, p0, p1, f0, f1 = w.rect
            for (q0, q1, g0, g1, kind, op) in self.acc.get(name, ()):
                if p0 < q1 and q0 < p1 and f0 < g1 and g0 < f1:
                    deps.append(op)
        return deps

    def _record(self, op, reads, writes):
        for w in writes:
            name, p0, p1, f0, f1 = w.rect
            lst = self.acc.setdefault(name, [])
            lst[:] = [a for a in lst if not (p0 <= a[0] and a[1] <= p1 and f0 <= a[2] and a[3] <= f1)]
            lst.append((p0, p1, f0, f1, 'w', op))
        for r in reads:
            name, p0, p1, f0, f1 = r.rect
            lst = self.acc.setdefault(name, [])
            if op.dma is None:
                lst[:] = [a for a in lst if not (a[4] == 'r' and a[5].eng == op.eng and a[5].dma is None
                                                 and p0 <= a[0] and a[1] <= p1 and f0 <= a[2] and a[3] <= f1)]
            lst.append((p0, p1, f0, f1, 'r', op))

    def op(self, eng, fn, reads=(), writes=(), dma=None):
        o = Op()
        o.eng = eng
        o.fn = fn
        o.signal = False
        o.token = None
        o.dma = dma
        o.seq = len(self.ops[eng])
        import sys as _s
        o.kind = _s._getframe(1).f_code.co_name
        need = {}
        for d in self._deps(reads, writes):
            if d.dma is not None:
                key = ('dma', d.dma)
                val = self.dma_cnt[d.dma]
            else:
                if d.eng == eng:
                    if eng == 'pe':
                        continue
                    if dma is None and (o.seq - d.seq) > 3:
                        continue
                key = ('eng', d.eng)
                val = d.seq
            if val > need.get(key, (-1, None))[0]:
                need[key] = (val, d)
        waits = []
        wd = self.waited[eng]
        for key, (val, d) in need.items():
            if wd.get(key, -1) >= val:
                continue
            wd[key] = val
            if key[0] == 'eng':
                d.signal = True
            waits.append((key, val, d))
        o.waits = waits
        if dma is not None:
            self.dma_cnt[dma] = self.dma_cnt.get(dma, 0) + 16
            o.token = self.dma_cnt[dma]
        self.ops[eng].append(o)
        self._record(o, reads, writes)
        return o

    def emit(self, final_waits=()):
        nc = self.nc
        for e in ENGS:
            c = 0
            for o in self.ops[e]:
                if o.dma is None and o.signal:
                    c += 1
                    o.token = c
        for e in ENGS:
            self.eng_sems[e] = self.es.enter_context(nc.semaphore('sem_' + e))
        for k in self.dma_cnt:
            self.dma_sems[k] = self.es.enter_context(nc.semaphore('dsem_' + k))
        block = self.es.enter_context(nc.Block())

        def run(e):
            def body(engine):
                for o in self.ops[e]:
                    for key, val, d in o.waits:
                        if key[0] == 'dma':
                            engine.wait_ge(self.dma_sems[key[1]], val)
                        else:
                            engine.wait_ge(self.eng_sems[key[1]], d.token)
                    ins = o.fn(engine)
                    if o.dma is not None:
                        ins.then_inc(self.dma_sems[o.dma], 16)
                    elif o.signal:
                        ins.then_inc(self.eng_sems[e], 1)
                if e == 'sp':
                    for k in final_waits:
                        engine.wait_ge(self.dma_sems[k], self.dma_cnt[k])
            return body
        block.tensor(run('pe'))
        block.vector(run('dve'))
        block.scalar(run('act'))
        block.gpsimd(run('pool'))
        block.sync(run('sp'))

    def mm(self, out, lhsT, rhs, start, stop):
        self.op('pe', lambda e: e.matmul(out.ap, lhsT=lhsT.ap, rhs=rhs.ap, start=start, stop=stop),
                reads=[lhsT, rhs], writes=[out])

    def tr(self, out, in_, ident):
        self.op('pe', lambda e: e.transpose(out=out.ap, in_=in_.ap, identity=ident.ap),
                reads=[in_, ident], writes=[out])

    def act(self, out, in_, func, bias=None, scale=None, eng='act'):
        rd = [in_]
        kw = {}
        if bias is not None:
            if isinstance(bias, View):
                rd.append(bias)
                kw['bias'] = bias.ap
            else:
                kw['bias'] = bias
        if scale is not None:
            if isinstance(scale, View):
                rd.append(scale)
                kw['scale'] = scale.ap
            else:
                kw['scale'] = scale
        self.op('act', lambda e: e.activation(out=out.ap, in_=in_.ap, func=func, **kw), reads=rd, writes=[out])

    def tt(self, out, a, b, op, eng='dve'):
        self.op(eng, lambda e: e.tensor_tensor(out=out.ap, in0=a.ap, in1=b.ap, op=op), reads=[a, b], writes=[out])

    def ts(self, out, a, s1, s2, op0, op1=None, eng='dve'):
        rd = [a]
        v1 = s1
        v2 = s2
        if isinstance(s1, View):
            rd.append(s1)
            v1 = s1.ap
        if isinstance(s2, View):
            rd.append(s2)
            v2 = s2.ap
        if op1 is None:
            self.op(eng, lambda e: e.tensor_scalar(out=out.ap, in0=a.ap, scalar1=v1, scalar2=None, op0=op0),
                    reads=rd, writes=[out])
        else:
            self.op(eng, lambda e: e.tensor_scalar(out=out.ap, in0=a.ap, scalar1=v1, scalar2=v2, op0=op0, op1=op1),
                    reads=rd, writes=[out])

    def stt(self, out, a, s, b, op0, op1, eng='dve'):
        rd = [a, b]
        sv = s
        if isinstance(s, View):
            rd.append(s)
            sv = s.ap
        self.op(eng, lambda e: e.scalar_tensor_tensor(out=out.ap, in0=a.ap, scalar=sv, in1=b.ap, op0=op0, op1=op1),
                reads=rd, writes=[out])

    def red(self, out, in_, op=ALU.add):
        self.op('dve', lambda e: e.tensor_reduce(out=out.ap, in_=in_.ap, axis=AX.X, op=op), reads=[in_], writes=[out])

    def copy(self, out, in_, eng='dve'):
        if eng == 'act':
            self.op('act', lambda e: e.activation(out=out.ap, in_=in_.ap, func=AF.Copy), reads=[in_], writes=[out])
        else:
            self.op(eng, lambda e: e.tensor_copy(out=out.ap, in_=in_.ap), reads=[in_], writes=[out])

    def recip(self, out, in_):
        self.op('dve', lambda e: e.reciprocal(out=out.ap, in_=in_.ap), reads=[in_], writes=[out])

    def memset(self, out, val, eng='dve'):
        self.op(eng, lambda e: e.memset(out.ap, val), writes=[out])

    def dma(self, q, out, in_, slot=None):
        if slot is None:
            self.nuniq = getattr(self, 'nuniq', 0) + 1
            slot = 'u%d' % self.nuniq
        self.op(q, lambda e: e.dma_start(out=out.ap, in_=in_.ap), reads=[in_], writes=[out], dma=slot)


def build(hasB, hasA, dbg=None, mode='all'):
    nc = bass.Bass("TRN2", target_bir_lowering=False)
    IN = {}

    def din(name, shape, dt=F32):
        IN[name] = dram(nc.dram_tensor(name, list(shape), dt, kind="ExternalInput").ap(), name)
        return IN[name]

    def dout(name, shape, dt=F32):
        return dram(nc.dram_tensor(name, list(shape), dt, kind="ExternalOutput").ap(), name)

    din('hT', [1024, TOK])
    din('pos', [1, TOK], I32)
    din('c_tab', [128, 16])
    din('c_mats', [128, 7 * 128])
    phases = []
    if hasB:
        phases.append('b')
    if hasA:
        phases.append('a')
    for ph in phases:
        din(ph + '_g1', [128, 8])
        din(ph + '_win_k2', [1024, 256])
        din(ph + '_win_v', [1024, 128])
        din(ph + '_win_zc', [1024, 256])
        din(ph + '_kg', [128, 1])
        din(ph + '_ssm3', [128, 24])
        din(ph + '_Bre', [128, 8 * 128])
        din(ph + '_Bim', [128, 8 * 128])
    if hasB:
        for nm in ('b_g2', 'b_g3', 'b_gmix'):
            din(nm, [128, 8])
        din('b_win_za', [1024, 512])
        din('b_win_q', [1024, 512])
        din('b_lngb', [128, 512])
        din('b_wsT', [128, 512])
        din('b_bs', [128, 4])
        din('b_qg', [128, 1])
        din('b_sinks', [128, 8])
        din('b_Cre', [128, 8 * 32])
        din('b_Cim', [128, 8 * 32])
        din('b_dskip', [128, 2])
        din('b_glu1', [256, 256])
        din('b_glu2', [256, 256])
        din('b_wout', [1024, 1024])
        din('b_wff1', [1024, 4096])
        din('b_wff2', [4096, 1024])
        din('b_wg', [1024, 1024])
        din('b_wp', [256, 1024])
        din('b_pT', [256, TOK])
        din('b_Gre', [128, 24])
        din('b_Gim', [128, 24])
        din('b_khalo', [128, 256])
        din('b_vhalo', [128, 128])
        din('b_flag', [128, 1])
        hT_out = dout('hT_out', [1024, TOK])
    if hasA:
        oF = dout('oF', [128, 16])
        okh = dout('okh', [128, 256])
        ovh = dout('ovh', [128, 128])
    dbg_out = {}
    if dbg:
        for nm, shp in dbg.items():
            dbg_out[nm] = dout('dbg_' + nm, shp)

    es = ExitStack()
    with es:
        P = Prog(nc, es)
        hb = P.sbuf('h', 8 * TOK, F32)
        xn = P.sbuf('xn', 8 * TOK, BF16)
        AR_N = 49 * 1024
        arena = P.sbuf('arena', AR_N, BF16)
        cst = P.sbuf('cst', 2 * 128 + 16, F32)
        cb = P.sbuf('cb', 7 * 128, BF16)
        small = P.sbuf('small', 2048, F32)
        ps = [P.psum_bank('ps%d' % i) for i in range(8)]

        class Arena:
            def __init__(self):
                self.off = 0
                self.peak = 0

            def alloc(self, n, dtype=BF16):
                nb = n * DSZ[dtype]
                nb = (nb + 3) // 4 * 4
                b = Buf('arena', arena.t, n, dtype, self.off, BF16)
                self.off += nb
                self.peak = max(self.peak, self.off)
                assert self.off <= AR_N * 2, ("arena overflow", self.off)
                return b
        AR = Arena()

        class Small:
            def __init__(self):
                self.off = 0

            def alloc(self, n):
                b = small.sub(self.off, n)
                self.off += n
                assert self.off <= 2048, self.off
                return b
        SM = Small()

        def hv(c, t0, t1):
            return hb.v(0, 128, c * TOK + t0, c * TOK + t1)

        def xv(c, t0, t1, p0=0, p1=128):
            return xn.v(p0, p1, c * TOK + t0, c * TOK + t1)

        SM_g = {}

        def pre_small(name, n):
            b = SM.alloc(n)
            P.dma('sp', b.v(), IN[name], 'pre')
            SM_g[name] = b
            return b
        ident_f = cst.sub(0, 128)
        mtri_f = cst.sub(128, 128)
        ctab = cst.sub(256, 16)
        P.dma('sp', ident_f.v(), IN['c_mats'].f(lambda a: a[:, 0:128]), 'pre')
        P.dma('sp', mtri_f.v(), IN['c_mats'].f(lambda a: a[:, 512:640]), 'pre')
        P.dma('sp', ctab.v(), IN['c_tab'], 'pre')
        for ph in phases:
            pre_small(ph + '_g1', 8)
            pre_small(ph + '_kg', 1)
            pre_small(ph + '_ssm3', 24)
        if hasB:
            for nm, n in (('b_g2', 8), ('b_g3', 8), ('b_gmix', 8), ('b_qg', 1), ('b_bs', 4), ('b_dskip', 2),
                          ('b_sinks', 8), ('b_Gre', 24), ('b_Gim', 24), ('b_flag', 1), ('b_khalo', 256), ('b_vhalo', 128)):
                pre_small(nm, n)
        P.dma('pool', cb.v(), IN['c_mats'], 'prec')
        ident_b = cb.sub(0, 128)
        ones_b = cb.sub(128, 128)
        bones_b = cb.sub(256, 128)
        prot_b = cb.sub(384, 128)
        mtri_b = cb.sub(512, 128)
        mcur_b = cb.sub(640, 128)
        mprev_b = cb.sub(768, 128)
        for c in range(8):
            P.dma('sp', hb.v(0, 128, c * TOK, (c + 1) * TOK), IN['hT'].f(lambda a, c=c: a[c * 128:(c + 1) * 128, :]), 'ldh%d' % c)

        halfpi = SM.alloc(1)
        P.memset(halfpi.v(), math.pi / 2.0)
        epsb = SM.alloc(1)
        P.memset(epsb.v(), EPS)

        def dump(nm, view):
            if nm in dbg_out:
                P.dma('sp' if view.ap.dtype == F32 else 'pool', dbg_out[nm], view, 'dbg')

        sq_ring = [AR.alloc(512) for _ in range(2)]
        ln_s = AR.alloc(512, F32)
        rstd_s = AR.alloc(512, F32)
        WR_N = 3
        WR_SZ = 2048
        wring = [AR.alloc(WR_SZ) for _ in range(WR_N)]
        wr_i = [0]
        base_mark = AR.off

        def range_reduce_sincos(cos_out, sin_out, ang, t_a, t_b, ki):
            P.ts(t_b, ang, 1.0 / TWO_PI, None, ALU.mult)
            P.copy(ki, t_b)
            P.copy(t_b, ki)
            P.stt(t_a, t_b, -TWO_PI, ang, ALU.mult, ALU.add)
            P.act(t_b, t_a, AF.Sin, scale=0.5)
            P.act(t_a, t_a, AF.Sin, scale=0.25)
            P.tt(t_a, t_a, t_a, ALU.mult)
            P.ts(t_a, t_a, -2.0, 1.0, ALU.mult, ALU.add)
            P.stt(sin_out, t_b, 2.0, t_a, ALU.mult, ALU.mult)
            P.tt(t_b, t_b, t_b, ALU.mult)
            P.ts(cos_out, t_b, -2.0, 1.0, ALU.mult, ALU.add)

        def make_rope(rope, t0, t1):
            n = t1 - t0
            mark = AR.off
            W_ = min(512, n)
            posi = AR.alloc(W_, I32)
            posf = AR.alloc(W_, F32)
            ta = AR.alloc(W_, F32)
            tb = AR.alloc(W_, F32)
            ki = AR.alloc(W_, I32)
            for o in range(0, n, W_):
                P.dma('sp', posi.v(), IN['pos'].f(lambda a, o=o: a[:, t0 + o:t0 + o + W_].partition_broadcast(128)), 'pos')
                P.copy(posf.v(), posi.v())
                P.ts(posf.v(), posf.v(), ctab.v(0, 128, 0, 1), None, ALU.mult)
                range_reduce_sincos(rope.v(0, 128, o, o + W_), rope.v(0, 128, n + o, n + o + W_), posf.v(),
                                    ta.v(), tb.v(), ki.v())
            AR.off = mark

        def rmsnorm_feat(g):
            k = 0
            for t in range(NT):
                t0, t1 = t * 512, (t + 1) * 512
                pb = ps[t % 2]
                for c in range(8):
                    sq = sq_ring[k % 2]
                    k += 1
                    P.act(sq.v(), hv(c, t0, t1), AF.Square)
                    P.mm(pb.v(), ones_b.v(), sq.v(), c == 0, c == 7)
                P.act(ln_s.v(), pb.v(), AF.Ln, scale=1.0 / 1024, bias=epsb.v())
                P.act(rstd_s.v(), ln_s.v(), AF.Exp, scale=-0.5)
                for c in range(8):
                    P.stt(xv(c, t0, t1), hv(c, t0, t1), g.v(0, 128, c, c + 1), rstd_s.v(), ALU.mult, ALU.mult)

        def load_w(wname, r0, nk, c0, ncols):
            slot = wr_i[0] % WR_N
            wr_i[0] += 1
            wb = wring[slot]
            assert nk * ncols <= WR_SZ
            src = IN[wname].f(lambda a: a[r0:r0 + nk * 128, c0:c0 + ncols].rearrange("(k p) n -> p k n", p=128))
            dst = wb.v(0, 128, 0, nk * ncols).f(lambda a: a.rearrange("p (k n) -> p k n", k=nk))
            P.dma('pool', dst, src, 'w%d' % slot)
            return wb

        def lin_fm(wname, nk, ncols_total, rhs_fn, evac_fn, r0=0, c_base=0, ntiles=NT, tile_w=512):
            mi = 0
            cw = WR_SZ // nk
            for c0 in range(0, ncols_total, cw):
                ncols = min(cw, ncols_total - c0)
                wb = load_w(wname, r0, nk, c_base + c0, ncols)
                for ml in range(ncols // 128):
                    m = c0 // 128 + ml
                    banks = [ps[(mi % 2) * 4 + t] for t in range(ntiles)]
                    mi += 1
                    for k in range(nk):
                        lv = wb.v(0, 128, k * ncols + ml * 128, k * ncols + (ml + 1) * 128)
                        for t in range(ntiles):
                            P.mm(banks[t].v(0, 128, 0, tile_w), lv, rhs_fn(k, t), k == 0, k == nk - 1)
                    for t in range(ntiles):
                        evac_fn(m, t, banks[t].v(0, 128, 0, tile_w))

        def ssm_scalars(ph):
            s3 = SM_g[ph + '_ssm3']
            are, aim, ldt = s3.sub(0, 8), s3.sub(8, 8), s3.sub(16, 8)
            T = [SM.alloc(8) for _ in range(8)]
            dt = T[0]
            P.act(dt.v(), ldt.v(), AF.Exp)
            th = T[1]
            P.tt(th.v(), aim.v(), dt.v(), ALU.mult)
            cs, sn = T[2], T[3]
            kib = SM.alloc(8)
            range_reduce_sincos(cs.v(), sn.v(), th.v(), T[4].v(), T[5].v(), kib.v().f(lambda a: a.bitcast(I32)))
            mag = T[6]
            P.tt(mag.v(), are.v(), dt.v(), ALU.mult)
            P.act(mag.v(), mag.v(), AF.Exp)
            pw = SM.alloc(16 * 17)

            def pre(j):
                return pw.sub(j * 16, 8)

            def pim(j):
                return pw.sub(j * 16 + 8, 8)
            P.memset(pre(0).v(), 1.0)
            P.memset(pim(0).v(), 0.0)
            P.tt(pre(1).v(), cs.v(), mag.v(), ALU.mult)
            P.tt(pim(1).v(), sn.v(), mag.v(), ALU.mult)
            ta, tb = T[4], T[5]

            def cmul(ore, oim, are_, aim_, bre, bim):
                P.tt(ta.v(), are_.v(), bre.v(), ALU.mult)
                P.tt(tb.v(), aim_.v(), bim.v(), ALU.mult)
                P.tt(ta.v(), ta.v(), tb.v(), ALU.subtract)
                P.tt(tb.v(), are_.v(), bim.v(), ALU.mult)
                P.tt(oim.v(), aim_.v(), bre.v(), ALU.mult)
                P.tt(oim.v(), oim.v(), tb.v(), ALU.add)
                P.copy(ore.v(), ta.v())
            for j in range(2, 9):
                cmul(pre(j), pim(j), pre(j - 1), pim(j - 1), pre(1), pim(1))
            for k in range(1, 9):
                cmul(pre(8 + k), pim(8 + k), pre(7 + k), pim(7 + k), pre(7 + k), pim(7 + k))
            n2 = T[7]
            P.tt(n2.v(), are.v(), are.v(), ALU.mult)
            P.tt(ta.v(), aim.v(), aim.v(), ALU.mult)
            P.tt(n2.v(), n2.v(), ta.v(), ALU.add)
            P.recip(n2.v(), n2.v())
            lm1 = T[0]
            P.ts(lm1.v(), pre(1).v(), -1.0, None, ALU.add)
            cre, cim = SM.alloc(8), SM.alloc(8)
            P.tt(ta.v(), lm1.v(), are.v(), ALU.mult)
            P.tt(tb.v(), pim(1).v(), aim.v(), ALU.mult)
            P.tt(ta.v(), ta.v(), tb.v(), ALU.add)
            P.tt(cre.v(), ta.v(), n2.v(), ALU.mult)
            P.tt(ta.v(), pim(1).v(), are.v(), ALU.mult)
            P.tt(tb.v(), lm1.v(), aim.v(), ALU.mult)
            P.tt(ta.v(), ta.v(), tb.v(), ALU.subtract)
            P.tt(cim.v(), ta.v(), n2.v(), ALU.mult)
            return dict(pw=pw, cre=cre, cim=cim)

        def ssm_half(ph, SC, half, zc, Ire, Iim, out):
            pw, cre, cim = SC['pw'], SC['cre'], SC['cim']
            tk = 0
            for pl in range(4):
                pr = half * 4 + pl
                m2 = AR.off
                Braw_re = AR.alloc(128, F32)
                Braw_im = AR.alloc(128, F32)
                Bb_re = AR.alloc(128, F32)
                Bb_im = AR.alloc(128, F32)
                nat_re = AR.alloc(128, F32)
                nat_im = AR.alloc(128, F32)
                tmpm = AR.alloc(128, F32)
                Lin = AR.alloc(8 * 2 * 128)
                xr = AR.alloc(256, F32)
                xi = AR.alloc(256, F32)
                yr = AR.alloc(256, F32)
                yi = AR.alloc(256, F32)
                tmp = AR.alloc(256, F32)
                P.dma('sp', Braw_re.v(), IN[ph + '_Bre'].f(lambda a, pr=pr: a[:, pr * 128:(pr + 1) * 128]), ph + 'bre')
                P.dma('sp', Braw_im.v(), IN[ph + '_Bim'].f(lambda a, pr=pr: a[:, pr * 128:(pr + 1) * 128]), ph + 'bim')
                crv, civ = cre.v(0, 128, pr, pr + 1), cim.v(0, 128, pr, pr + 1)
                P.ts(tmpm.v(), Braw_im.v(), civ, None, ALU.mult)
                P.stt(Bb_re.v(), Braw_re.v(), crv, tmpm.v(), ALU.mult, ALU.subtract)
                P.ts(tmpm.v(), Braw_im.v(), crv, None, ALU.mult)
                P.stt(Bb_im.v(), Braw_re.v(), civ, tmpm.v(), ALU.mult, ALU.add)
                if out is not None:
                    Cre = AR.alloc(32, F32)
                    Cim = AR.alloc(32, F32)
                    Cimn = AR.alloc(32, F32)
                    tmc = AR.alloc(32, F32)
                    P.dma('sp', Cre.v(), IN[ph + '_Cre'].f(lambda a, pr=pr: a[:, pr * 32:(pr + 1) * 32]), ph + 'cre')
                    P.dma('sp', Cim.v(), IN[ph + '_Cim'].f(lambda a, pr=pr: a[:, pr * 32:(pr + 1) * 32]), ph + 'cim')
                    P.ts(Cimn.v(), Cim.v(), -1.0, None, ALU.mult)
                for tau in range(8):
                    j = 7 - tau
                    prv, piv = pw.sub(j * 16, 8).v(0, 128, pr, pr + 1), pw.sub(j * 16 + 8, 8).v(0, 128, pr, pr + 1)
                    P.ts(tmpm.v(), Bb_im.v(), piv, None, ALU.mult)
                    P.stt(nat_re.v(), Bb_re.v(), prv, tmpm.v(), ALU.mult, ALU.subtract)
                    P.ts(tmpm.v(), Bb_im.v(), prv, None, ALU.mult)
                    P.stt(nat_im.v(), Bb_re.v(), piv, tmpm.v(), ALU.mult, ALU.add)
                    for part, nat in ((0, nat_re), (1, nat_im)):
                        pb = ps[tk % 4]
                        tk += 1
                        P.tr(pb.v(0, 128, 0, 128), nat.v(), ident_f.v())
                        o = (tau * 2 + part) * 128
                        P.copy(Lin.v(0, 128, o, o + 128), pb.v(0, 128, 0, 128), eng='act')
                    if out is not None:
                        pb = ps[4 + (tk % 2)]
                        ov = pb.v(0, 128, 0, 32)
                        P.mm(ov, nat_re.v(), Cre.v(), True, False)
                        P.mm(ov, nat_im.v(), Cimn.v(), False, True)
                        o = j * 128 + pl * 32
                        P.copy(out['Lk'].v(0, 128, o, o + 32), ov, eng='act')
                        jj = tau + 1
                        p2r = pw.sub(jj * 16, 8).v(0, 128, pr, pr + 1)
                        p2i = pw.sub(jj * 16 + 8, 8).v(0, 128, pr, pr + 1)
                        q2 = pl % 2
                        o_re = ((pl * 8 + tau) * 2 + 0) * 64 + q2 * 32
                        o_im = ((pl * 8 + tau) * 2 + 1) * 64 + q2 * 32
                        H = out['H']
                        P.ts(tmc.v(), Cim.v(), p2i, None, ALU.mult)
                        P.stt(H.v(0, 128, o_re, o_re + 32), Cre.v(), p2r, tmc.v(), ALU.mult, ALU.subtract)
                        P.ts(tmc.v(), Cim.v(), p2r, -1.0, ALU.mult, ALU.mult)
                        P.stt(H.v(0, 128, o_im, o_im + 32), Cre.v(), p2i, tmc.v(), ALU.mult, ALU.subtract)
                        P.ts(H.v(0, 128, o_im, o_im + 32), H.v(0, 128, o_im, o_im + 32), -1.0, None, ALU.mult)
                for part, xx in ((0, xr), (1, xi)):
                    pb = ps[6 + part]
                    for tau in range(8):
                        o = (tau * 2 + part) * 128
                        rhs = zc.v(0, 128, half * TOK, (half + 1) * TOK).f(
                            lambda a, tau=tau: a.rearrange("p (c t) -> p c t", t=8)[:, :, tau])
                        P.mm(pb.v(0, 128, 0, 256), Lin.v(0, 128, o, o + 128), rhs, tau == 0, tau == 7)
                    P.copy(xx.v(), pb.v(0, 128, 0, 256), eng='act')
                if Ire is not None:
                    a_re, a_im = pw.sub(8 * 16, 8).v(0, 128, pr, pr + 1), pw.sub(8 * 16 + 8, 8).v(0, 128, pr, pr + 1)
                    i_re, i_im = Ire.v(0, 128, pr, pr + 1), Iim.v(0, 128, pr, pr + 1)
                    x0r, x0i, t0_ = xr.v(0, 128, 0, 1), xi.v(0, 128, 0, 1), tmp.v(0, 128, 0, 1)
                    P.stt(x0r, i_re, a_re, x0r, ALU.mult, ALU.add)
                    P.ts(t0_, i_im, a_im, None, ALU.mult)
                    P.tt(x0r, x0r, t0_, ALU.subtract)
                    P.stt(x0i, i_re, a_im, x0i, ALU.mult, ALU.add)
                    P.stt(x0i, i_im, a_re, x0i, ALU.mult, ALU.add)
                src_r, src_i, dst_r, dst_i = xr, xi, yr, yi
                for k in range(8):
                    d = 1 << k
                    a_re = pw.sub((8 + k) * 16, 8).v(0, 128, pr, pr + 1)
                    a_im = pw.sub((8 + k) * 16 + 8, 8).v(0, 128, pr, pr + 1)
                    n = 256 - d
                    P.copy(dst_r.v(0, 128, 0, d), src_r.v(0, 128, 0, d), eng='act')
                    P.copy(dst_i.v(0, 128, 0, d), src_i.v(0, 128, 0, d), eng='act')
                    P.stt(tmp.v(0, 128, 0, n), src_r.v(0, 128, 0, n), a_re, src_r.v(0, 128, d, 256), ALU.mult, ALU.add)
                    P.ts(dst_r.v(0, 128, d, 256), src_i.v(0, 128, 0, n), a_im, None, ALU.mult)
                    P.tt(dst_r.v(0, 128, d, 256), tmp.v(0, 128, 0, n), dst_r.v(0, 128, d, 256), ALU.subtract)
                    P.stt(tmp.v(0, 128, 0, n), src_r.v(0, 128, 0, n), a_im, src_i.v(0, 128, d, 256), ALU.mult, ALU.add)
                    P.stt(dst_i.v(0, 128, d, 256), src_i.v(0, 128, 0, n), a_re, tmp.v(0, 128, 0, n), ALU.mult, ALU.add)
                    src_r, dst_r = dst_r, src_r
                    src_i, dst_i = dst_i, src_i
                assert src_r is xr
                if out is None:
                    Fo = SC['Fo']
                    P.copy(Fo.v(0, 128, pr, pr + 1), xr.v(0, 128, 255, 256), eng='act')
                    P.copy(Fo.v(0, 128, 8 + pr, 8 + pr + 1), xi.v(0, 128, 255, 256), eng='act')
                else:
                    Xp = out['Xp']
                    for part, xx, Iv in ((0, xr, Ire), (1, xi, Iim)):
                        o = (pl * 2 + part) * 256
                        P.copy(Xp.v(0, 128, o + 1, o + 256), xx.v(0, 128, 0, 255), eng='act')
                        P.copy(Xp.v(0, 128, o, o + 1), Iv.v(0, 128, pr, pr + 1), eng='act')
                AR.off = m2

        def qk_process(out_v, src_ps, gview, rope, r0, n, scr):
            qg, sq, rs = scr
            RW = rope.ncols // 2
            P.act(sq.v(0, 128, 0, n), src_ps, AF.Square)
            pb2 = ps[6]
            P.mm(pb2.v(0, 128, 0, n), bones_b.v(), sq.v(0, 128, 0, n), True, True)
            P.act(rs.v(0, 128, 0, n), pb2.v(0, 128, 0, n), AF.Ln, scale=1.0 / 64, bias=epsb.v())
            P.act(rs.v(0, 128, 0, n), rs.v(0, 128, 0, n), AF.Exp, scale=-0.5)
            P.stt(qg.v(0, 128, 0, n), src_ps, gview, rs.v(0, 128, 0, n), ALU.mult, ALU.mult)
            pb3 = ps[7]
            P.mm(pb3.v(0, 128, 0, n), prot_b.v(), qg.v(0, 128, 0, n), True, True)
            P.tt(sq.v(0, 128, 0, n), qg.v(0, 128, 0, n), rope.v(0, 128, r0, r0 + n), ALU.mult)
            P.tt(qg.v(0, 128, 0, n), pb3.v(0, 128, 0, n), rope.v(0, 128, RW + r0, RW + r0 + n), ALU.mult)
            P.tt(out_v, sq.v(0, 128, 0, n), qg.v(0, 128, 0, n), ALU.add)

        def lin_fm6(wname, nk, ncols_total, rhs_fn, evac_fn, **kw):
            mi = 0
            cw = WR_SZ // nk
            ntiles = kw.get('ntiles', NT)
            tile_w = kw.get('tile_w', 512)
            for c0 in range(0, ncols_total, cw):
                ncols = min(cw, ncols_total - c0)
                wb = load_w(wname, 0, nk, c0, ncols)
                for ml in range(ncols // 128):
                    m = c0 // 128 + ml
                    for t in range(ntiles):
                        bank = ps[mi % 6]
                        mi += 1
                        for k in range(nk):
                            lv = wb.v(0, 128, k * ncols + ml * 128, k * ncols + (ml + 1) * 128)
                            P.mm(bank.v(0, 128, 0, tile_w), lv, rhs_fn(k, t), k == 0, k == nk - 1)
                        evac_fn(m, t, bank.v(0, 128, 0, tile_w))

        import os as _os
        STOP = 'h1' if mode == 'mix' else _os.environ.get('K_STOP', '')
        SKIP_MIX = (mode == 'ffn') or bool(_os.environ.get('K_SKIP_MIX'))

        def phaseB():
                esink = SM.alloc(8)
                P.act(esink.v(), SM_g['b_sinks'].v(), AF.Exp)
                nb0 = SM.alloc(1)
                P.ts(nb0.v(), SM_g['b_flag'].v(), -1.0, 30000.0, ALU.add, ALU.mult)
                rmsnorm_feat(SM_g['b_g1'])
                dump('xn1', xn.v(0, 128, 0, 512))
                if STOP == 'xn1':
                    return
                def add_to_h(m, t, pv):
                    P.tt(hv(m, t * 512, (t + 1) * 512), hv(m, t * 512, (t + 1) * 512), pv, ALU.add)

                def mixers():
                    yT = AR.alloc(8 * TOK)
                    mixer_mark = AR.off

                    zc = AR.alloc(2 * TOK)
                    lin_fm('b_win_zc', 8, 256, lambda k, t: xv(k, t * 512, (t + 1) * 512),
                           lambda m, t, pv: P.copy(zc.v(0, 128, m * TOK + t * 512, m * TOK + (t + 1) * 512), pv, eng='act'))
                    if STOP == 'zc':
                        return True
                    SC = ssm_scalars('b')
                    pw = SC['pw']
                    Ire, Iim = SM.alloc(8), SM.alloc(8)
                    a_re, a_im = pw.sub(16 * 16, 8), pw.sub(16 * 16 + 8, 8)
                    Gre, Gim = SM_g['b_Gre'], SM_g['b_Gim']
                    tA, tB = SM.alloc(8), SM.alloc(8)
                    P.copy(Ire.v(), Gre.v(0, 128, 0, 8))
                    P.copy(Iim.v(), Gim.v(0, 128, 0, 8))
                    for s in (1, 2):
                        P.tt(tA.v(), Ire.v(), a_re.v(), ALU.mult)
                        P.tt(tB.v(), Iim.v(), a_im.v(), ALU.mult)
                        P.tt(tA.v(), tA.v(), tB.v(), ALU.subtract)
                        P.tt(tB.v(), Ire.v(), a_im.v(), ALU.mult)
                        P.tt(Iim.v(), Iim.v(), a_re.v(), ALU.mult)
                        P.tt(Iim.v(), Iim.v(), tB.v(), ALU.add)
                        P.tt(Iim.v(), Iim.v(), Gim.v(0, 128, s * 8, s * 8 + 8), ALU.add)
                        P.tt(Ire.v(), tA.v(), Gre.v(0, 128, s * 8, s * 8 + 8), ALU.add)
                    yg = AR.alloc(2 * TOK)
                    half_mark = AR.off
                    dsk = SM_g['b_dskip']
                    kk = 0
                    for half in range(2):
                        out = dict(Xp=AR.alloc(4 * 2 * 256), H=AR.alloc(4 * 8 * 2 * 64), Lk=AR.alloc(8 * 128))
                        P.memset(out['H'].v(), 0.0)
                        ssm_half('b', SC, half, zc, Ire, Iim, out)
                        if STOP == 'st':
                            return True
                        Xp, H, Lk = out['Xp'], out['H'], out['Lk']
                        ysc = [AR.alloc(512, F32) for _ in range(2)]
                        for t in range(NT):
                            pb = ps[kk % 4]
                            kk += 1
                            zt = zc.v(0, 128, half * TOK + t * 512, half * TOK + (t + 1) * 512)
                            for lag in range(8):
                                outv = pb.v().f(lambda a, lag=lag: a.rearrange("p (c t) -> p c t", t=8)[:, :, lag:8])
                                rhs = zt.f(lambda a, lag=lag: a.rearrange("p (c t) -> p c t", t=8)[:, :, 0:8 - lag])
                                P.mm(outv, Lk.v(0, 128, lag * 128, (lag + 1) * 128), rhs, lag == 0, False)
                            for tau in range(8):
                                for pl in range(4):
                                    qd = pl // 2
                                    for part in range(2):
                                        o = ((pl * 8 + tau) * 2 + part) * 64
                                        xo = (pl * 2 + part) * 256 + t * 64
                                        outv = pb.v(qd * 64, qd * 64 + 64).f(
                                            lambda a, tau=tau: a.rearrange("p (c t) -> p c t", t=8)[:, :, tau])
                                        last = (tau == 7 and pl % 2 == 1 and part == 1)
                                        P.mm(outv, H.v(0, 128, o, o + 64), Xp.v(0, 128, xo, xo + 64), False, last)
                            ys = ysc[kk % 2]
                            P.stt(ys.v(), zt, dsk.v(0, 128, half, half + 1), pb.v(), ALU.mult, ALU.add)
                            P.act(yg.v(0, 128, half * TOK + t * 512, half * TOK + (t + 1) * 512), ys.v(), AF.Gelu)
                        AR.off = half_mark
                    dump('yg', yg.v(0, 128, 0, 512))
                    if STOP == 'yg':
                        return True
                    gl1 = AR.alloc(2 * 256)
                    gl2 = AR.alloc(2 * 256)
                    for nm, gb in (('b_glu1', gl1), ('b_glu2', gl2)):
                        P.dma('pool', gb.v().f(lambda a: a.rearrange("p (k n) -> p k n", k=2)),
                              IN[nm].f(lambda a: a.rearrange("(k p) n -> p k n", p=128)), nm)
                    if STOP == 'g1':
                        return True
                    ycf = AR.alloc(2 * 512, F32)
                    sgs = AR.alloc(512, F32)
                    ysq = AR.alloc(512)
                    for t in range(NT):
                        t0, t1 = t * 512, (t + 1) * 512
                        for m in range(2):
                            p1_, p2_ = ps[4 + m * 2], ps[5 + m * 2]
                            for k in range(2):
                                rhs = yg.v(0, 128, k * TOK + t0, k * TOK + t1)
                                P.mm(p1_.v(), gl1.v(0, 128, k * 256 + m * 128, k * 256 + (m + 1) * 128), rhs, k == 0, k == 1)
                            for k in range(2):
                                rhs = yg.v(0, 128, k * TOK + t0, k * TOK + t1)
                                P.mm(p2_.v(), gl2.v(0, 128, k * 256 + m * 128, k * 256 + (m + 1) * 128), rhs, k == 0, k == 1)
                            P.act(sgs.v(), p2_.v(), AF.Sigmoid)
                            P.tt(ycf.v(0, 128, m * 512, (m + 1) * 512), p1_.v(), sgs.v(), ALU.mult)
                        if STOP == 'g2':
                            return True
                        pbs = ps[(t % 2)]
                        for m in range(2):
                            P.act(ysq.v(), ycf.v(0, 128, m * 512, (m + 1) * 512), AF.Square)
                            P.mm(pbs.v(), ones_b.v(), ysq.v(), m == 0, m == 1)
                        P.act(ln_s.v(), pbs.v(), AF.Ln, scale=1.0 / 256, bias=epsb.v())
                        P.act(rstd_s.v(), ln_s.v(), AF.Exp, scale=-0.5)
                        for m in range(2):
                            P.stt(yT.v(0, 128, (6 + m) * TOK + t0, (6 + m) * TOK + t1), ycf.v(0, 128, m * 512, (m + 1) * 512),
                                  SM_g['b_gmix'].v(0, 128, 6 + m, 7 + m), rstd_s.v(), ALU.mult, ALU.mult)
                    dump('yc', yT.v(0, 128, 6 * TOK, 6 * TOK + 512))
                    if STOP == 'yc':
                        return True
                    AR.off = mixer_mark

                    zag = AR.alloc(16 * 512)
                    wsT = AR.alloc(512)
                    lngb = AR.alloc(512)
                    P.dma('pool', wsT.v(), IN['b_wsT'], 'wsT')
                    for h in range(4):
                        P.tt(wsT.v(0, 128, h * 128, (h + 1) * 128), wsT.v(0, 128, h * 128, (h + 1) * 128), mtri_b.v(), ALU.mult)
                    P.dma('pool', lngb.v(), IN['b_lngb'], 'lngb')
                    for grp in range(4):
                        for kq in range(4):
                            wb = load_w('b_win_za', kq * 256, 2, 0, 512)
                            for bi in range(4):
                                blk = grp * 4 + bi
                                pb = ps[bi]
                                for k2 in range(2):
                                    k = kq * 2 + k2
                                    P.mm(pb.v(), xv(k, blk * 128, (blk + 1) * 128), wb.v(0, 128, k2 * 512, (k2 + 1) * 512),
                                         k == 0, k == 7)
                        for bi in range(4):
                            blk = grp * 4 + bi
                            P.act(zag.v(0, 128, blk * 512, (blk + 1) * 512), ps[bi].v(), AF.Gelu)
                    vsel = lambda a: a.rearrange("p (c h x) -> p c h x", c=16, h=4, x=128)[:, :, :, 64:128]
                    usel = lambda a: a.rearrange("p (c h x) -> p c h x", c=16, h=4, x=128)[:, :, :, 0:64]
                    vn = AR.alloc(16 * 256)
                    vn4 = lambda a: a.rearrange("p (c h d) -> p c h d", c=16, h=4, d=64)
                    st1 = SM.alloc(64)
                    st2 = SM.alloc(64)
                    st3 = SM.alloc(64)
                    bc64 = lambda a: a.rearrange("p (c h) -> p c h", c=16).unsqueeze(3).to_broadcast([128, 16, 4, 64])
                    st3d = lambda a: a.rearrange("p (c h) -> p c h", c=16)
                    P.red(st1.v().f(st3d), zag.v().f(vsel))
                    P.tt(vn.v().f(vn4), zag.v().f(vsel), zag.v().f(vsel), ALU.mult)
                    P.red(st2.v().f(st3d), vn.v().f(vn4))
                    P.ts(st1.v(), st1.v(), 1.0 / 64, None, ALU.mult)
                    P.tt(st3.v(), st1.v(), st1.v(), ALU.mult)
                    P.stt(st2.v(), st2.v(), 1.0 / 64, st3.v(), ALU.mult, ALU.subtract)
                    P.act(st2.v(), st2.v(), AF.Sqrt, bias=epsb.v())
                    P.recip(st2.v(), st2.v())
                    P.tt(vn.v().f(vn4), zag.v().f(vsel), st1.v().f(bc64), ALU.subtract)
                    P.tt(vn.v().f(vn4), vn.v().f(vn4), st2.v().f(bc64), ALU.mult)
                    gsel = lambda a: a.rearrange("p (h d) -> p h d", h=4).unsqueeze(1).to_broadcast([128, 16, 4, 64])
                    P.tt(vn.v().f(vn4), vn.v().f(vn4), lngb.v(0, 128, 0, 256).f(gsel), ALU.mult)
                    P.tt(vn.v().f(vn4), vn.v().f(vn4), lngb.v(0, 128, 256, 512).f(gsel), ALU.add)
                    ya = AR.alloc(16 * 256)
                    ya4 = lambda a: a.rearrange("p (c h d) -> p c h d", c=16, h=4, d=64)
                    for h in range(4):
                        for cc in range(2):
                            pb = ps[4 + (h * 2 + cc) % 4]
                            rhs = vn.v().f(lambda a, h=h, cc=cc: vn4(a)[:, cc * 8:(cc + 1) * 8, h, :])
                            outv = pb.v().f(lambda a: a.rearrange("p (c d) -> p c d", d=64))
                            P.mm(outv, wsT.v(0, 128, h * 128, (h + 1) * 128), rhs, True, True)
                            uv = zag.v().f(lambda a, h=h, cc=cc: usel(a)[:, cc * 8:(cc + 1) * 8, h, :])
                            ov = ya.v().f(lambda a, h=h, cc=cc: ya4(a)[:, cc * 8:(cc + 1) * 8, h, :])
                            P.stt(ov, outv, SM_g['b_bs'].v(0, 128, h, h + 1), uv, ALU.add, ALU.mult)
                    yasq = AR.alloc(16 * 256)
                    sa = SM.alloc(16)
                    y3 = lambda a: a.rearrange("p (c f) -> p c f", c=16)
                    P.tt(yasq.v().f(y3), ya.v().f(y3), ya.v().f(y3), ALU.mult)
                    P.red(sa.v(), yasq.v().f(y3))
                    P.act(sa.v(), sa.v(), AF.Sqrt, scale=1.0 / 256, bias=epsb.v())
                    P.recip(sa.v(), sa.v())
                    P.tt(yasq.v().f(y3), ya.v().f(y3), sa.v().f(lambda a: a.unsqueeze(2).to_broadcast([128, 16, 256])), ALU.mult)
                    k = 0
                    for blk in range(16):
                        for m in range(2):
                            pb = ps[k % 4]
                            k += 1
                            pv = pb.v(0, 128, 0, 64).f(lambda a: a.bitcast(BF16))
                            P.tr(pv, yasq.v(0, 128, blk * 256 + m * 128, blk * 256 + (m + 1) * 128), ident_b.v())
                            P.act(yT.v(0, 128, m * TOK + blk * 128, m * TOK + (blk + 1) * 128), pv, AF.Copy,
                                  scale=SM_g['b_gmix'].v(0, 128, m, m + 1))
                    dump('ya', yT.v(0, 128, 0, 512))
                    if STOP == 'ya':
                        return True
                    AR.off = mixer_mark

                    rope = AR.alloc(2 * TOK)
                    make_rope(rope, 0, TOK)
                    qf = AR.alloc(4 * TOK)
                    KW = TOK + 128
                    kf = AR.alloc(2 * KW)
                    vt = AR.alloc(17 * 260)
                    scr = (AR.alloc(512), AR.alloc(512), AR.alloc(512, F32))
                    for kh in range(2):
                        P.copy(kf.v(0, 128, kh * KW, kh * KW + 128), SM_g['b_khalo'].v(0, 128, kh * 128, (kh + 1) * 128))
                    P.memset(vt.v(), 1.0)
                    v4 = lambda a: a.rearrange("p (b k x) -> p b k x", b=17, k=2, x=130)
                    P.copy(vt.v().f(lambda a: v4(a)[:, 0, :, 0:64]),
                           SM_g['b_vhalo'].v().f(lambda a: a.rearrange("p (k d) -> p k d", k=2)))
                    if STOP == 'b1':
                        return True
                    lin_fm6('b_win_q', 8, 512, lambda k, t: xv(k, t * 512, (t + 1) * 512),
                            lambda m, t, pv: qk_process(qf.v(0, 128, m * TOK + t * 512, m * TOK + (t + 1) * 512), pv,
                                                        SM_g['b_qg'].v(), rope, t * 512, 512, scr))
                    lin_fm6('b_win_k2', 8, 256, lambda k, t: xv(k, t * 512, (t + 1) * 512),
                            lambda m, t, pv: qk_process(kf.v(0, 128, m * KW + 128 + t * 512, m * KW + 128 + (t + 1) * 512), pv,
                                                        SM_g['b_kg'].v(), rope, t * 512, 512, scr))
                    wv = load_w('b_win_v', 0, 8, 0, 128)
                    for blk in range(16):
                        pb = ps[blk % 4]
                        for k in range(8):
                            P.mm(pb.v(0, 128, 0, 128), xv(k, blk * 128, (blk + 1) * 128), wv.v(0, 128, k * 128, (k + 1) * 128),
                                 k == 0, k == 7)
                        P.copy(vt.v().f(lambda a, blk=blk: v4(a)[:, blk + 1, :, 0:64]),
                               pb.v(0, 128, 0, 128).f(lambda a: a.rearrange("p (k d) -> p k d", k=2)), eng='act')
                    dump('qf', qf.v(0, 128, 0, 512))
                    dump('kf', kf.v(0, 128, 128, 640))
                    if STOP == 'b2':
                        return True
                    xoff = [0]

                    def xalloc(n, dtype=BF16):
                        b = Buf('xn', xn.t, n, dtype, xoff[0], BF16)
                        xoff[0] += n * DSZ[dtype]
                        return b
                    pexp = [xalloc(512) for _ in range(4)]
                    ybt = [xalloc(512, F32) for _ in range(2)]
                    ybn = [xalloc(512) for _ in range(2)]
                    ysq2 = xalloc(512)
                    dn = SM.alloc(16)
                    ssq = SM.alloc(4)
                    pk = 0
                    for blk in range(16):
                        yb = ybt[blk % 2]
                        for kh in range(2):
                            po = ps[4 + (blk * 2 + kh) % 2]
                            pes = []
                            for kb in range(2):
                                pSe, pSo = ps[kb * 2], ps[kb * 2 + 1]
                                pe_ = pexp[pk % 4]
                                pk += 1
                                pes.append(pe_)
                                kcol = kh * KW + (blk + kb) * 128
                                for hh in range(4):
                                    c = kh * 2 + hh // 2
                                    p0 = (hh % 2) * 64
                                    pS = pSe if hh % 2 == 0 else pSo
                                    j2 = hh // 2
                                    P.mm(pS.v(0, 128, j2 * 128, (j2 + 1) * 128), kf.v(p0, p0 + 64, kcol, kcol + 128),
                                         qf.v(p0, p0 + 64, c * TOK + blk * 128, c * TOK + (blk + 1) * 128), True, True)
                                for par, pS in ((0, pSe), (1, pSo)):
                                    ov = pe_.v().f(lambda a, par=par: a.rearrange("p (a b q) -> p a b q", a=2, b=2)[:, :, par, :])
                                    iv = pS.v(0, 128, 0, 256).f(lambda a: a.rearrange("p (a q) -> p a q", a=2))
                                    if kb == 0 and blk == 0:
                                        P.act(ov, iv, AF.Exp, scale=0.125, bias=nb0.v())
                                    else:
                                        P.act(ov, iv, AF.Exp, scale=0.125)
                                msk = (mprev_b if kb == 0 else mcur_b).v().f(lambda a: a.unsqueeze(1).to_broadcast([128, 4, 128]))
                                P.tt(pe_.v().f(lambda a: a.rearrange("p (h q) -> p h q", h=4)),
                                     pe_.v().f(lambda a: a.rearrange("p (h q) -> p h q", h=4)), msk, ALU.mult)
                            for hh in range(4):
                                for kb in range(2):
                                    rhs = vt.v().f(lambda a, blk=blk, kb=kb, kh=kh: v4(a)[:, blk + kb, kh, 0:65])
                                    P.mm(po.v(0, 128, hh * 65, (hh + 1) * 65), pes[kb].v(0, 128, hh * 128, (hh + 1) * 128), rhs,
                                         kb == 0, kb == 1)
                            o4 = lambda a: a.rearrange("p (h x) -> p h x", x=65)
                            dv = dn.v(0, 128, kh * 4, kh * 4 + 4)
                            P.tt(dv.f(lambda a: a.unsqueeze(2)), po.v(0, 128, 0, 260).f(lambda a: o4(a)[:, :, 64:65]),
                                 esink.v(0, 128, kh * 4, kh * 4 + 4).f(lambda a: a.unsqueeze(2)), ALU.add)
                            P.recip(dv, dv)
                            P.tt(yb.v(0, 128, kh * 256, (kh + 1) * 256).f(lambda a: a.rearrange("p (h d) -> p h d", h=4)),
                                 po.v(0, 128, 0, 260).f(lambda a: o4(a)[:, :, 0:64]),
                                 dv.f(lambda a: a.unsqueeze(2).to_broadcast([128, 4, 64])), ALU.mult)
                        if STOP == 'b3':
                            return True
                        sv = ssq.v(0, 128, blk % 4, blk % 4 + 1)
                        P.tt(ysq2.v(), yb.v(), yb.v(), ALU.mult)
                        P.red(sv, ysq2.v())
                        P.act(sv, sv, AF.Sqrt, scale=1.0 / 512, bias=epsb.v())
                        P.recip(sv, sv)
                        yn = ybn[blk % 2]
                        P.ts(yn.v(), yb.v(), sv, None, ALU.mult)
                        for m in range(4):
                            pb = ps[6 + (blk * 4 + m) % 2]
                            pv = pb.v(0, 128, 0, 64).f(lambda a: a.bitcast(BF16))
                            P.tr(pv, yn.v(0, 128, m * 128, (m + 1) * 128), ident_b.v())
                            P.act(yT.v(0, 128, (2 + m) * TOK + blk * 128, (2 + m) * TOK + (blk + 1) * 128), pv, AF.Copy,
                                  scale=SM_g['b_gmix'].v(0, 128, 2 + m, 3 + m))
                    dump('yb', yT.v(0, 128, 2 * TOK, 2 * TOK + 512))
                    if STOP == 'yb':
                        return True
                    AR.off = mixer_mark

                    lin_fm('b_wout', 8, 1024, lambda k, t: yT.v(0, 128, k * TOK + t * 512, k * TOK + (t + 1) * 512), add_to_h)
                    dump('h1', hb.v(0, 128, 0, 512))
                    if STOP == 'h1':
                        return True
                    AR.off = base_mark


                if not SKIP_MIX:
                    if mixers():
                        return
                AR.off = base_mark
                rmsnorm_feat(SM_g['b_g2'])
                hid = [AR.alloc(4 * TOK) for _ in range(2)]
                rl = [AR.alloc(512, F32) for _ in range(3)]
                rk = [0]
                for g in range(int(_os.environ.get('K_FFN_G', '8'))):
                    hd = hid[g % 2]

                    def ev1(m, t, pv, hd=hd):
                        r = rl[rk[0] % 3]
                        rk[0] += 1
                        P.act(r.v(), pv, AF.Relu)
                        P.tt(hd.v(0, 128, m * TOK + t * 512, m * TOK + (t + 1) * 512), r.v(), r.v(), ALU.mult)
                    lin_fm('b_wff1', 8, 512, lambda k, t: xv(k, t * 512, (t + 1) * 512), ev1, c_base=g * 512)
                    lin_fm('b_wff2', 4, 1024, lambda k, t, hd=hd: hd.v(0, 128, k * TOK + t * 512, k * TOK + (t + 1) * 512),
                           add_to_h, r0=g * 512)
                dump('h2', hb.v(0, 128, 0, 512))
                if STOP == 'h2':
                    return
                AR.off = base_mark

                rmsnorm_feat(SM_g['b_g3'])
                pT = AR.alloc(2 * TOK)
                P.dma('pool', pT.v().f(lambda a: a.rearrange("p (k n) -> p k n", k=2)),
                      IN['b_pT'].f(lambda a: a.rearrange("(k p) n -> p k n", p=128)), 'pT')
                wp = AR.alloc(2 * 1024)
                P.dma('pool', wp.v().f(lambda a: a.rearrange("p (k n) -> p k n", k=2)),
                      IN['b_wp'].f(lambda a: a.rearrange("(k p) n -> p k n", p=128)), 'wp')
                gs = [AR.alloc(512, F32) for _ in range(2)]
                gk = [0]

                def ev_ple(m, t, pv):
                    g_ = gs[gk[0] % 2]
                    gk[0] += 1
                    P.act(g_.v(), pv, AF.Sigmoid)
                    for k in range(2):
                        P.mm(pv, wp.v(0, 128, k * 1024 + m * 128, k * 1024 + (m + 1) * 128),
                             pT.v(0, 128, k * TOK + t * 512, k * TOK + (t + 1) * 512), k == 0, k == 1)
                    P.tt(g_.v(), pv, g_.v(), ALU.mult)
                    P.tt(hv(m, t * 512, (t + 1) * 512), hv(m, t * 512, (t + 1) * 512), g_.v(), ALU.add)
                lin_fm('b_wg', 8, 1024, lambda k, t: xv(k, t * 512, (t + 1) * 512), ev_ple)
        if hasB:
            phaseB()
            AR.off = base_mark
            for c in range(8):
                P.dma('sp', hT_out.f(lambda a, c=c: a[c * 128:(c + 1) * 128, :]), hb.v(0, 128, c * TOK, (c + 1) * TOK), 'out')

        if hasA:
            rmsnorm_feat(SM_g['a_g1'])
            zc = AR.alloc(2 * TOK)
            lin_fm('a_win_zc', 8, 256, lambda k, t: xv(k, t * 512, (t + 1) * 512),
                   lambda m, t, pv: P.copy(zc.v(0, 128, m * TOK + t * 512, m * TOK + (t + 1) * 512), pv, eng='act'))
            SC = ssm_scalars('a')
            SC['Fo'] = SM.alloc(16)
            for half in range(2):
                ssm_half('a', SC, half, zc, None, None, None)
            P.dma('sp', oF, SC['Fo'].v(), 'out')
            rope = AR.alloc(2 * 128)
            t0 = TOK - 128
            make_rope(rope, t0, TOK)
            scr = (AR.alloc(512), AR.alloc(512), AR.alloc(512, F32))
            kho = AR.alloc(256)
            khf = AR.alloc(256, F32)
            lin_fm6('a_win_k2', 8, 256, lambda k, t: xv(k, t0, TOK),
                    lambda m, t, pv: qk_process(kho.v(0, 128, m * 128, (m + 1) * 128), pv, SM_g['a_kg'].v(), rope, 0, 128, scr),
                    ntiles=1, tile_w=128)
            P.copy(khf.v(), kho.v())
            P.dma('sp', okh, khf.v(), 'out')
            wv = load_w('a_win_v', 0, 8, 0, 128)
            pb = ps[0]
            for k in range(8):
                P.mm(pb.v(0, 128, 0, 128), xv(k, t0, TOK), wv.v(0, 128, k * 128, (k + 1) * 128), k == 0, k == 7)
            vho = AR.alloc(128, F32)
            P.copy(vho.v(), pb.v(0, 128, 0, 128), eng='act')
            P.dma('sp', ovh, vho.v(), 'out')
        fin = ['out'] + (['dbg'] if dbg_out else [])
        print("arena peak KB", AR.peak / 1024, "small", SM.off, "ops", {e: len(P.ops[e]) for e in ENGS})
        P.emit(final_waits=fin)
    return nc


def _consts():
    p = np.arange(128)
    inv = (1.0 / (10000.0 ** (np.arange(0, 64, 2, dtype=np.float32) / 64))).astype(np.float32)
    tab = np.zeros((128, 16), np.float32)
    tab[:, 0] = inv[p % 32]
    ident = np.eye(128, dtype=np.float32)
    ones = np.ones((128, 128), np.float32)
    bones = np.zeros((128, 128), np.float32)
    bones[:64, :64] = 1
    bones[64:, 64:] = 1
    prot = np.zeros((128, 128), np.float32)
    for m in range(128):
        if (m % 64) < 32:
            prot[m + 32, m] = -1.0
        else:
            prot[m - 32, m] = 1.0
    tri = (p[:, None] <= p[None, :]).astype(np.float32)
    mcur = tri.copy()
    mprev = (p[:, None] > p[None, :]).astype(np.float32)
    mats = np.concatenate([ident, ones, bones, prot, tri, mcur, mprev], axis=1)
    return tab, np.ascontiguousarray(mats)


def _chunkcols(v):
    return np.ascontiguousarray(v.reshape(8, 128).T)


def _layer_inputs(ph, L, W, full):
    d = {}
    d[ph + '_g1'] = _chunkcols(W['attn_norm_g'][L])
    win = W['w_in'][L]
    k = win[:, 1024:1152]
    d[ph + '_win_k2'] = np.ascontiguousarray(np.concatenate([k[:, 0:64], k[:, 0:64], k[:, 64:128], k[:, 64:128]], axis=1))
    d[ph + '_win_v'] = np.ascontiguousarray(win[:, 1152:1280])
    d[ph + '_win_zc'] = np.ascontiguousarray(win[:, 1280:1536])
    d[ph + '_kg'] = np.ascontiguousarray(np.tile(W['k_norm_g'][L], 2).reshape(128, 1))
    def nat(a):
        return np.ascontiguousarray(a.reshape(8, 2, 64).transpose(1, 2, 0).reshape(128, 8))
    ldt = np.repeat(W['ssm_log_dt'][L][:, None], 64, axis=1)
    d[ph + '_ssm3'] = np.concatenate([nat(W['ssm_a_re'][L]), nat(W['ssm_a_im'][L]), nat(ldt)], axis=1)

    def bbig(b):
        out = np.zeros((128, 8, 128), np.float32)
        for pr in range(8):
            for gl in range(2):
                g = pr * 2 + gl
                gh = g % 8
                out[gl * 64:(gl + 1) * 64, pr, gh * 16:(gh + 1) * 16] = b[g]
        return out.reshape(128, 1024)
    d[ph + '_Bre'] = bbig(W['ssm_b_re'][L])
    d[ph + '_Bim'] = bbig(W['ssm_b_im'][L])
    if not full:
        return d
    d['b_g2'] = _chunkcols(W['mlp_norm_g'][L])
    d['b_g3'] = _chunkcols(W['ple_norm_g'][L])
    d['b_gmix'] = _chunkcols(W['mix_out_g'][L])
    d['b_win_za'] = np.ascontiguousarray(win[:, 0:512])
    d['b_win_q'] = np.ascontiguousarray(win[:, 512:1024])
    lng = np.broadcast_to(W['gmlp_ln_g'][L].reshape(1, 256), (128, 256))
    lnb = np.broadcast_to(W['gmlp_ln_b'][L].reshape(1, 256), (128, 256))
    d['b_lngb'] = np.ascontiguousarray(np.concatenate([lng, lnb], axis=1))
    d['b_wsT'] = np.ascontiguousarray(W['gmlp_ws'][L].transpose(2, 0, 1).reshape(128, 512))
    d['b_bs'] = np.ascontiguousarray(W['gmlp_bs'][L].T)
    d['b_qg'] = np.ascontiguousarray(np.tile(W['q_norm_g'][L], 2).reshape(128, 1))
    d['b_sinks'] = np.ascontiguousarray(np.broadcast_to(W['sinks'][L].reshape(1, 8), (128, 8)))

    def cnat(c):
        out = np.zeros((128, 8, 32), np.float32)
        for pr in range(8):
            for gl in range(2):
                g = pr * 2 + gl
                out[gl * 64:(gl + 1) * 64, pr, gl * 16:(gl + 1) * 16] = c[g].T
        return out.reshape(128, 256)
    d['b_Cre'] = cnat(W['ssm_c_re'][L])
    d['b_Cim'] = cnat(W['ssm_c_im'][L])
    d['b_dskip'] = np.ascontiguousarray(W['ssm_d'][L].reshape(2, 128).T)
    for nm, key in (('b_glu1', 'glu_w1'), ('b_glu2', 'glu_w2'), ('b_wout', 'w_out'), ('b_wff1', 'w_ff1'),
                    ('b_wff2', 'w_ff2'), ('b_wg', 'w_ple_gate'), ('b_wp', 'w_ple_proj')):
        d[nm] = np.ascontiguousarray(W[key][L])
    return d


_PROGS = {}


def _prog(hasB, hasA, mode):
    key = (hasB, hasA, mode)
    if key not in _PROGS:
        _PROGS[key] = build(hasB, hasA, None, mode)
    return _PROGS[key]


def kernel(**inputs):
    W = {k: np.asarray(v) for k, v in inputs.items()}
    x = W['x']
    p = W['p']
    pos = W['positions']
    NC = 8
    tab, mats = _consts()
    hT = [np.ascontiguousarray(x[c // 4, (c % 4) * TOK:(c % 4 + 1) * TOK, :].T) for c in range(NC)]
    posc = [np.ascontiguousarray(pos[c // 4, (c % 4) * TOK:(c % 4 + 1) * TOK].reshape(1, TOK)).astype(np.int32)
            for c in range(NC)]
    zero_g = np.zeros((128, 24), np.float32)
    zero_k = np.zeros((128, 256), np.float32)
    zero_v = np.zeros((128, 128), np.float32)
    zero_f = np.zeros((128, 1), np.float32)
    one_f = np.ones((128, 1), np.float32)

    def launch(hasB, hasA, mode, Lb, La, exch):
        nc = _prog(hasB, hasA, mode)
        shared = {'c_tab': tab, 'c_mats': mats}
        if hasB:
            shared.update(_layer_inputs('b', Lb, W, True))
        if hasA:
            shared.update(_layer_inputs('a', La, W, False))
        in_maps = []
        for c in range(NC):
            m = dict(shared)
            m['hT'] = hT[c]
            m['pos'] = posc[c]
            if hasB:
                b, q = c // 4, c % 4
                m['b_pT'] = np.ascontiguousarray(p[Lb, b, q * TOK:(q + 1) * TOK, :].T)
                gre, gim = zero_g, zero_g
                kh, vh, fl = zero_k, zero_v, zero_f
                if exch is not None:
                    gre = np.zeros((128, 24), np.float32)
                    gim = np.zeros((128, 24), np.float32)
                    for s in range(3):
                        src = q - 3 + s
                        if src >= 0:
                            F = exch[b * 4 + src]['oF']
                            gre[:, s * 8:(s + 1) * 8] = F[:, 0:8]
                            gim[:, s * 8:(s + 1) * 8] = F[:, 8:16]
                    if q > 0:
                        kh, vh, fl = exch[c - 1]['okh'], exch[c - 1]['ovh'], one_f
                m['b_Gre'], m['b_Gim'] = gre, gim
                m['b_khalo'], m['b_vhalo'], m['b_flag'] = kh, vh, fl
            in_maps.append(m)
        return run_bass_kernel_spmd(nc, in_maps, core_ids=list(range(NC)))

    res = launch(False, True, 'all', None, 0, None)
    exch = [{k: np.asarray(res.results[c][k]) for k in ('oF', 'okh', 'ovh')} for c in range(NC)]
    for L in range(4):
        res = launch(True, False, 'mix', L, None, exch)
        hT = [np.asarray(res.results[c]['hT_out']) for c in range(NC)]
        hasA = L < 3
        res = launch(True, hasA, 'ffn', L, L + 1 if hasA else None, None)
        hT = [np.asarray(res.results[c]['hT_out']) for c in range(NC)]
        if hasA:
            exch = [{k: np.asarray(res.results[c][k]) for k in ('oF', 'okh', 'ovh')} for c in range(NC)]
    out = np.empty_like(x)
    for c in range(NC):
        out[c // 4, (c % 4) * TOK:(c % 4 + 1) * TOK, :] = hT[c].T
    return out
```

```python
import math
import numpy as np
from contextlib import ExitStack
import concourse.bass as bass
import concourse.mybir as mybir
from concourse.bass_utils import run_bass_kernel_spmd

F32 = mybir.dt.float32
BF16 = mybir.dt.bfloat16
I32 = mybir.dt.int32
ALU = mybir.AluOpType
AF = mybir.ActivationFunctionType
AX = mybir.AxisListType
DSZ = {F32: 4, BF16: 2, I32: 4}
ENGS = ['pe', 'dve', 'act', 'pool', 'sp']

TOK = 2048
NT = 4
EPS = 1e-6
TWO_PI = 2.0 * math.pi


class View:
    __slots__ = ('ap', 'rect')

    def __init__(self, ap, rect):
        self.ap = ap
        self.rect = rect

    def f(self, fn):
        return View(fn(self.ap), self.rect)


class Buf:
    def __init__(self, name, t, ncols, dtype, byte_off=0, tdtype=None):
        self.name = name
        self.t = t
        self.ncols = ncols
        self.dtype = dtype
        self.esz = DSZ[dtype]
        self.byte_off = byte_off
        self.tdtype = tdtype if tdtype is not None else dtype

    def v(self, p0=0, p1=128, c0=0, c1=None):
        if c1 is None:
            c1 = self.ncols
        assert 0 <= p0 < p1 <= 128 and 0 <= c0 < c1 <= self.ncols, (self.name, p0, p1, c0, c1)
        b0 = self.byte_off + c0 * self.esz
        b1 = self.byte_off + c1 * self.esz
        tsz = DSZ[self.tdtype]
        ap = self.t[p0:p1, b0 // tsz:b1 // tsz]
        if self.tdtype != self.dtype:
            ap = ap.bitcast(self.dtype)
        return View(ap, (self.name, p0, p1, b0, b1))

    def sub(self, c0, ncols, dtype=None):
        dtype = dtype or self.dtype
        return Buf(self.name, self.t, ncols, dtype, self.byte_off + c0 * self.esz, self.tdtype)


def dram(ap, name):
    return View(ap, (name, 0, 128, 0, 1 << 40))


class Op:
    __slots__ = ('eng', 'fn', 'waits', 'signal', 'token', 'seq', 'dma', 'kind')


class Prog:
    def __init__(self, nc, es):
        self.nc = nc
        self.es = es
        self.ops = {e: [] for e in ENGS}
        self.acc = {}
        self.waited = {e: {} for e in ENGS}
        self.dma_cnt = {}
        self.dma_sems = {}
        self.eng_sems = {}

    def sbuf(self, name, ncols, dtype):
        t = self.es.enter_context(self.nc.sbuf_tensor(name, [128, ncols], dtype))
        return Buf(name, t, ncols, dtype)

    def psum_bank(self, name):
        t = self.es.enter_context(self.nc.psum_tensor(name, [128, 512], F32))
        return Buf(name, t, 512, F32)

    def _deps(self, reads, writes):
        deps = []
        for r in reads:
            name, p0, p1, f0, f1 = r.rect
            for (q0, q1, g0, g1, kind, op) in self.acc.get(name, ()):
                if kind == 'w' and p0 < q1 and q0 < p1 and f0 < g1 and g0 < f1:
                    deps.append(op)
        for w in writes:
            name, p0, p1, f0, f1 = w.rect
            for (q0, q1, g0, g1, kind, op) in self.acc.get(name, ()):
                if p0 < q1 and q0 < p1 and f0 < g1 and g0 < f1:
                    deps.append(op)
        return deps

    def _record(self, op, reads, writes):
        for w in writes:
            name, p0, p1, f0, f1 = w.rect
            lst = self.acc.setdefault(name, [])
            lst[:] = [a for a in lst if not (p0 <= a[0] and a[1] <= p1 and f0 <= a[2] and a[3] <= f1)]
            lst.append((p0, p1, f0, f1, 'w', op))
        for r in reads:
            name, p0, p1, f0, f1 = r.rect
            lst = self.acc.setdefault(name, [])
            if op.dma is None:
                lst[:] = [a for a in lst if not (a[4] == 'r' and a[5].eng == op.eng and a[5].dma is None
                                                 and p0 <= a[0] and a[1] <= p1 and f0 <= a[2] and a[3] <= f1)]
            lst.append((p0, p1, f0, f1, 'r', op))

    def op(self, eng, fn, reads=(), writes=(), dma=None):
        o = Op()
        o.eng = eng
        o.fn = fn
        o.signal = False
        o.token = None
        o.dma = dma
        o.seq = len(self.ops[eng])
        import sys as _s
        o.kind = _s._getframe(1).f_code.co_name
        need = {}
        for d in self._deps(reads, writes):
            if d.dma is not None:
                key = ('dma', d.dma)
                val = self.dma_cnt[d.dma]
            else:
                if d.eng == eng:
                    if eng == 'pe':
                        continue
                    if dma is None and (o.seq - d.seq) > 3:
                        continue
                key = ('eng', d.eng)
                val = d.seq
            if val > need.get(key, (-1, None))[0]:
                need[key] = (val, d)
        waits = []
        wd = self.waited[eng]
        for key, (val, d) in need.items():
            if wd.get(key, -1) >= val:
                continue
            wd[key] = val
            if key[0] == 'eng':
                d.signal = True
            waits.append((key, val, d))
        o.waits = waits
        if dma is not None:
            self.dma_cnt[dma] = self.dma_cnt.get(dma, 0) + 16
            o.token = self.dma_cnt[dma]
        self.ops[eng].append(o)
        self._record(o, reads, writes)
        return o

    def emit(self, final_waits=()):
        nc = self.nc
        for e in ENGS:
            c = 0
            for o in self.ops[e]:
                if o.dma is None and o.signal:
                    c += 1
                    o.token = c
        for e in ENGS:
            self.eng_sems[e] = self.es.enter_context(nc.semaphore('sem_' + e))
        for k in self.dma_cnt:
            self.dma_sems[k] = self.es.enter_context(nc.semaphore('dsem_' + k))
        block = self.es.enter_context(nc.Block())

        def run(e):
            def body(engine):
                for o in self.ops[e]:
                    for key, val, d in o.waits:
                        if key[0] == 'dma':
                            engine.wait_ge(self.dma_sems[key[1]], val)
                        else:
                            engine.wait_ge(self.eng_sems[key[1]], d.token)
                    ins = o.fn(engine)
                    if o.dma is not None:
                        ins.then_inc(self.dma_sems[o.dma], 16)
                    elif o.signal:
                        ins.then_inc(self.eng_sems[e], 1)
                if e == 'sp':
                    for k in final_waits:
                        engine.wait_ge(self.dma_sems[k], self.dma_cnt[k])
            return body
        block.tensor(run('pe'))
        block.vector(run('dve'))
        block.scalar(run('act'))
        block.gpsimd(run('pool'))
        block.sync(run('sp'))

    def mm(self, out, lhsT, rhs, start, stop):
        self.op('pe', lambda e: e.matmul(out.ap, lhsT=lhsT.ap, rhs=rhs.ap, start=start, stop=stop),
                reads=[lhsT, rhs], writes=[out])

    def tr(self, out, in_, ident):
        self.op('pe', lambda e: e.transpose(out=out.ap, in_=in_.ap, identity=ident.ap),
                reads=[in_, ident], writes=[out])

    def act(self, out, in_, func, bias=None, scale=None, eng='act'):
        rd = [in_]
        kw = {}
        if bias is not None:
            if isinstance(bias, View):
                rd.append(bias)
                kw['bias'] = bias.ap
            else:
                kw['bias'] = bias
        if scale is not None:
            if isinstance(scale, View):
                rd.append(scale)
                kw['scale'] = scale.ap
            else:
                kw['scale'] = scale
        self.op('act', lambda e: e.activation(out=out.ap, in_=in_.ap, func=func, **kw), reads=rd, writes=[out])

    def tt(self, out, a, b, op, eng='dve'):
        self.op(eng, lambda e: e.tensor_tensor(out=out.ap, in0=a.ap, in1=b.ap, op=op), reads=[a, b], writes=[out])

    def ts(self, out, a, s1, s2, op0, op1=None, eng='dve'):
        rd = [a]
        v1 = s1
        v2 = s2
        if isinstance(s1, View):
            rd.append(s1)
            v1 = s1.ap
        if isinstance(s2, View):
            rd.append(s2)
            v2 = s2.ap
        if op1 is None:
            self.op(eng, lambda e: e.tensor_scalar(out=out.ap, in0=a.ap, scalar1=v1, scalar2=None, op0=op0),
                    reads=rd, writes=[out])
        else:
            self.op(eng, lambda e: e.tensor_scalar(out=out.ap, in0=a.ap, scalar1=v1, scalar2=v2, op0=op0, op1=op1),
                    reads=rd, writes=[out])

    def stt(self, out, a, s, b, op0, op1, eng='dve'):
        rd = [a, b]
        sv = s
        if isinstance(s, View):
            rd.append(s)
            sv = s.ap
        self.op(eng, lambda e: e.scalar_tensor_tensor(out=out.ap, in0=a.ap, scalar=sv, in1=b.ap, op0=op0, op1=op1),
                reads=rd, writes=[out])

    def red(self, out, in_, op=ALU.add):
        self.op('dve', lambda e: e.tensor_reduce(out=out.ap, in_=in_.ap, axis=AX.X, op=op), reads=[in_], writes=[out])

    def copy(self, out, in_, eng='dve'):
        if eng == 'act':
            self.op('act', lambda e: e.activation(out=out.ap, in_=in_.ap, func=AF.Copy), reads=[in_], writes=[out])
        else:
            self.op(eng, lambda e: e.tensor_copy(out=out.ap, in_=in_.ap), reads=[in_], writes=[out])

    def recip(self, out, in_):
        self.op('dve', lambda e: e.reciprocal(out=out.ap, in_=in_.ap), reads=[in_], writes=[out])

    def memset(self, out, val, eng='dve'):
        self.op(eng, lambda e: e.memset(out.ap, val), writes=[out])

    def dma(self, q, out, in_, slot=None):
        if slot is None:
            self.nuniq = getattr(self, 'nuniq', 0) + 1
            slot = 'u%d' % self.nuniq
        self.op(q, lambda e: e.dma_start(out=out.ap, in_=in_.ap), reads=[in_], writes=[out], dma=slot)


def build(hasB, hasA, dbg=None, mode='all'):
    nc = bass.Bass("TRN2", target_bir_lowering=False)
    IN = {}

    def din(name, shape, dt=F32):
        IN[name] = dram(nc.dram_tensor(name, list(shape), dt, kind="ExternalInput").ap(), name)
        return IN[name]

    def dout(name, shape, dt=F32):
        return dram(nc.dram_tensor(name, list(shape), dt, kind="ExternalOutput").ap(), name)

    din('hT', [1024, TOK])
    din('pos', [1, TOK], I32)
    din('c_tab', [128, 16])
    din('c_mats', [128, 7 * 128])
    phases = []
    if hasB:
        phases.append('b')
    if hasA:
        phases.append('a')
    for ph in phases:
        din(ph + '_g1', [128, 8])
        din(ph + '_win_k2', [1024, 256])
        din(ph + '_win_v', [1024, 128])
        din(ph + '_win_zc', [1024, 256])
        din(ph + '_kg', [128, 1])
        din(ph + '_ssm3', [128, 24])
        din(ph + '_Bre', [128, 8 * 128])
        din(ph + '_Bim', [128, 8 * 128])
    if hasB:
        for nm in ('b_g2', 'b_g3', 'b_gmix'):
            din(nm, [128, 8])
        din('b_win_za', [1024, 512])
        din('b_win_q', [1024, 512])
        din('b_lngb', [128, 512])
        din('b_wsT', [128, 512])
        din('b_bs', [128, 4])
        din('b_qg', [128, 1])
        din('b_sinks', [128, 8])
        din('b_Cre', [128, 8 * 32])
        din('b_Cim', [128, 8 * 32])
        din('b_dskip', [128, 2])
        din('b_glu1', [256, 256])
        din('b_glu2', [256, 256])
        din('b_wout', [1024, 1024])
        din('b_wff1', [1024, 4096])
        din('b_wff2', [4096, 1024])
        din('b_wg', [1024, 1024])
        din('b_wp', [256, 1024])
        din('b_pT', [256, TOK])
        din('b_Gre', [128, 24])
        din('b_Gim', [128, 24])
        din('b_khalo', [128, 256])
        din('b_vhalo', [128, 128])
        din('b_flag', [128, 1])
        hT_out = dout('hT_out', [1024, TOK])
    if hasA:
        oF = dout('oF', [128, 16])
        okh = dout('okh', [128, 256])
        ovh = dout('ovh', [128, 128])
    dbg_out = {}
    if dbg:
        for nm, shp in dbg.items():
            dbg_out[nm] = dout('dbg_' + nm, shp)

    es = ExitStack()
    with es:
        P = Prog(nc, es)
        hb = P.sbuf('h', 8 * TOK, F32)
        xn = P.sbuf('xn', 8 * TOK, BF16)
        AR_N = 49 * 1024
        arena = P.sbuf('arena', AR_N, BF16)
        cst = P.sbuf('cst', 2 * 128 + 16, F32)
        cb = P.sbuf('cb', 7 * 128, BF16)
        small = P.sbuf('small', 2048, F32)
        ps = [P.psum_bank('ps%d' % i) for i in range(8)]

        class Arena:
            def __init__(self):
                self.off = 0
                self.peak = 0

            def alloc(self, n, dtype=BF16):
                nb = n * DSZ[dtype]
                nb = (nb + 3) // 4 * 4
                b = Buf('arena', arena.t, n, dtype, self.off, BF16)
                self.off += nb
                self.peak = max(self.peak, self.off)
                assert self.off <= AR_N * 2, ("arena overflow", self.off)
                return b
        AR = Arena()

        class Small:
            def __init__(self):
                self.off = 0

            def alloc(self, n):
                b = small.sub(self.off, n)
                self.off += n
                assert self.off <= 2048, self.off
                return b
        SM = Small()

        def hv(c, t0, t1):
            return hb.v(0, 128, c * TOK + t0, c * TOK + t1)

        def xv(c, t0, t1, p0=0, p1=128):
            return xn.v(p0, p1, c * TOK + t0, c * TOK + t1)

        SM_g = {}

        def pre_small(name, n):
            b = SM.alloc(n)
            P.dma('sp', b.v(), IN[name], 'pre')
            SM_g[name] = b
            return b
        ident_f = cst.sub(0, 128)
        mtri_f = cst.sub(128, 128)
        ctab = cst.sub(256, 16)
        P.dma('sp', ident_f.v(), IN['c_mats'].f(lambda a: a[:, 0:128]), 'pre')
        P.dma('sp', mtri_f.v(), IN['c_mats'].f(lambda a: a[:, 512:640]), 'pre')
        P.dma('sp', ctab.v(), IN['c_tab'], 'pre')
        for ph in phases:
            pre_small(ph + '_g1', 8)
            pre_small(ph + '_kg', 1)
            pre_small(ph + '_ssm3', 24)
        if hasB:
            for nm, n in (('b_g2', 8), ('b_g3', 8), ('b_gmix', 8), ('b_qg', 1), ('b_bs', 4), ('b_dskip', 2),
                          ('b_sinks', 8), ('b_Gre', 24), ('b_Gim', 24), ('b_flag', 1), ('b_khalo', 256), ('b_vhalo', 128)):
                pre_small(nm, n)
        P.dma('pool', cb.v(), IN['c_mats'], 'prec')
        ident_b = cb.sub(0, 128)
        ones_b = cb.sub(128, 128)
        bones_b = cb.sub(256, 128)
        prot_b = cb.sub(384, 128)
        mtri_b = cb.sub(512, 128)
        mcur_b = cb.sub(640, 128)
        mprev_b = cb.sub(768, 128)
        for c in range(8):
            P.dma('sp', hb.v(0, 128, c * TOK, (c + 1) * TOK), IN['hT'].f(lambda a, c=c: a[c * 128:(c + 1) * 128, :]), 'ldh%d' % c)

        halfpi = SM.alloc(1)
        P.memset(halfpi.v(), math.pi / 2.0)
        epsb = SM.alloc(1)
        P.memset(epsb.v(), EPS)

        def dump(nm, view):
            if nm in dbg_out:
                P.dma('sp' if view.ap.dtype == F32 else 'pool', dbg_out[nm], view, 'dbg')

        sq_ring = [AR.alloc(512) for _ in range(2)]
        ln_s = AR.alloc(512, F32)
        rstd_s = AR.alloc(512, F32)
        WR_N = 3
        WR_SZ = 2048
        wring = [AR.alloc(WR_SZ) for _ in range(WR_N)]
        wr_i = [0]
        base_mark = AR.off

        def range_reduce_sincos(cos_out, sin_out, ang, t_a, t_b, ki):
            P.ts(t_b, ang, 1.0 / TWO_PI, None, ALU.mult)
            P.copy(ki, t_b)
            P.copy(t_b, ki)
            P.stt(t_a, t_b, -TWO_PI, ang, ALU.mult, ALU.add)
            P.act(t_b, t_a, AF.Sin, scale=0.5)
            P.act(t_a, t_a, AF.Sin, scale=0.25)
            P.tt(t_a, t_a, t_a, ALU.mult)
            P.ts(t_a, t_a, -2.0, 1.0, ALU.mult, ALU.add)
            P.stt(sin_out, t_b, 2.0, t_a, ALU.mult, ALU.mult)
            P.tt(t_b, t_b, t_b, ALU.mult)
            P.ts(cos_out, t_b, -2.0, 1.0, ALU.mult, ALU.add)

        def make_rope(rope, t0, t1):
            n = t1 - t0
            mark = AR.off
            W_ = min(512, n)
            posi = AR.alloc(W_, I32)
            posf = AR.alloc(W_, F32)
            ta = AR.alloc(W_, F32)
            tb = AR.alloc(W_, F32)
            ki = AR.alloc(W_, I32)
            for o in range(0, n, W_):
                P.dma('sp', posi.v(), IN['pos'].f(lambda a, o=o: a[:, t0 + o:t0 + o + W_].partition_broadcast(128)), 'pos')
                P.copy(posf.v(), posi.v())
                P.ts(posf.v(), posf.v(), ctab.v(0, 128, 0, 1), None, ALU.mult)
                range_reduce_sincos(rope.v(0, 128, o, o + W_), rope.v(0, 128, n + o, n + o + W_), posf.v(),
                                    ta.v(), tb.v(), ki.v())
            AR.off = mark

        def rmsnorm_feat(g):
            k = 0
            for t in range(NT):
                t0, t1 = t * 512, (t + 1) * 512
                pb = ps[t % 2]
                for c in range(8):
                    sq = sq_ring[k % 2]
                    k += 1
                    P.act(sq.v(), hv(c, t0, t1), AF.Square)
                    P.mm(pb.v(), ones_b.v(), sq.v(), c == 0, c == 7)
                P.act(ln_s.v(), pb.v(), AF.Ln, scale=1.0 / 1024, bias=epsb.v())
                P.act(rstd_s.v(), ln_s.v(), AF.Exp, scale=-0.5)
                for c in range(8):
                    P.stt(xv(c, t0, t1), hv(c, t0, t1), g.v(0, 128, c, c + 1), rstd_s.v(), ALU.mult, ALU.mult)

        def load_w(wname, r0, nk, c0, ncols):
            slot = wr_i[0] % WR_N
            wr_i[0] += 1
            wb = wring[slot]
            assert nk * ncols <= WR_SZ
            src = IN[wname].f(lambda a: a[r0:r0 + nk * 128, c0:c0 + ncols].rearrange("(k p) n -> p k n", p=128))
            dst = wb.v(0, 128, 0, nk * ncols).f(lambda a: a.rearrange("p (k n) -> p k n", k=nk))
            P.dma('pool', dst, src, 'w%d' % slot)
            return wb

        def lin_fm(wname, nk, ncols_total, rhs_fn, evac_fn, r0=0, c_base=0, ntiles=NT, tile_w=512):
            mi = 0
            cw = WR_SZ // nk
            for c0 in range(0, ncols_total, cw):
                ncols = min(cw, ncols_total - c0)
                wb = load_w(wname, r0, nk, c_base + c0, ncols)
                for ml in range(ncols // 128):
                    m = c0 // 128 + ml
                    banks = [ps[(mi % 2) * 4 + t] for t in range(ntiles)]
                    mi += 1
                    for k in range(nk):
                        lv = wb.v(0, 128, k * ncols + ml * 128, k * ncols + (ml + 1) * 128)
                        for t in range(ntiles):
                            P.mm(banks[t].v(0, 128, 0, tile_w), lv, rhs_fn(k, t), k == 0, k == nk - 1)
                    for t in range(ntiles):
                        evac_fn(m, t, banks[t].v(0, 128, 0, tile_w))

        def ssm_scalars(ph):
            s3 = SM_g[ph + '_ssm3']
            are, aim, ldt = s3.sub(0, 8), s3.sub(8, 8), s3.sub(16, 8)
            T = [SM.alloc(8) for _ in range(8)]
            dt = T[0]
            P.act(dt.v(), ldt.v(), AF.Exp)
            th = T[1]
            P.tt(th.v(), aim.v(), dt.v(), ALU.mult)
            cs, sn = T[2], T[3]
            kib = SM.alloc(8)
            range_reduce_sincos(cs.v(), sn.v(), th.v(), T[4].v(), T[5].v(), kib.v().f(lambda a: a.bitcast(I32)))
            mag = T[6]
            P.tt(mag.v(), are.v(), dt.v(), ALU.mult)
            P.act(mag.v(), mag.v(), AF.Exp)
            pw = SM.alloc(16 * 17)

            def pre(j):
                return pw.sub(j * 16, 8)

            def pim(j):
                return pw.sub(j * 16 + 8, 8)
            P.memset(pre(0).v(), 1.0)
            P.memset(pim(0).v(), 0.0)
            P.tt(pre(1).v(), cs.v(), mag.v(), ALU.mult)
            P.tt(pim(1).v(), sn.v(), mag.v(), ALU.mult)
            ta, tb = T[4], T[5]

            def cmul(ore, oim, are_, aim_, bre, bim):
                P.tt(ta.v(), are_.v(), bre.v(), ALU.mult)
                P.tt(tb.v(), aim_.v(), bim.v(), ALU.mult)
                P.tt(ta.v(), ta.v(), tb.v(), ALU.subtract)
                P.tt(tb.v(), are_.v(), bim.v(), ALU.mult)
                P.tt(oim.v(), aim_.v(), bre.v(), ALU.mult)
                P.tt(oim.v(), oim.v(), tb.v(), ALU.add)
                P.copy(ore.v(), ta.v())
            for j in range(2, 9):
                cmul(pre(j), pim(j), pre(j - 1), pim(j - 1), pre(1), pim(1))
            for k in range(1, 9):
                cmul(pre(8 + k), pim(8 + k), pre(7 + k), pim(7 + k), pre(7 + k), pim(7 + k))
            n2 = T[7]
            P.tt(n2.v(), are.v(), are.v(), ALU.mult)
            P.tt(ta.v(), aim.v(), aim.v(), ALU.mult)
            P.tt(n2.v(), n2.v(), ta.v(), ALU.add)
            P.recip(n2.v(), n2.v())
            lm1 = T[0]
            P.ts(lm1.v(), pre(1).v(), -1.0, None, ALU.add)
            cre, cim = SM.alloc(8), SM.alloc(8)
            P.tt(ta.v(), lm1.v(), are.v(), ALU.mult)
            P.tt(tb.v(), pim(1).v(), aim.v(), ALU.mult)
            P.tt(ta.v(), ta.v(), tb.v(), ALU.add)
            P.tt(cre.v(), ta.v(), n2.v(), ALU.mult)
            P.tt(ta.v(), pim(1).v(), are.v(), ALU.mult)
            P.tt(tb.v(), lm1.v(), aim.v(), ALU.mult)
            P.tt(ta.v(), ta.v(), tb.v(), ALU.subtract)
            P.tt(cim.v(), ta.v(), n2.v(), ALU.mult)
            return dict(pw=pw, cre=cre, cim=cim)

        def ssm_half(ph, SC, half, zc, Ire, Iim, out):
            pw, cre, cim = SC['pw'], SC['cre'], SC['cim']
            tk = 0
            for pl in range(4):
                pr = half * 4 + pl
                m2 = AR.off
                Braw_re = AR.alloc(128, F32)
                Braw_im = AR.alloc(128, F32)
                Bb_re = AR.alloc(128, F32)
                Bb_im = AR.alloc(128, F32)
                nat_re = AR.alloc(128, F32)
                nat_im = AR.alloc(128, F32)
                tmpm = AR.alloc(128, F32)
                Lin = AR.alloc(8 * 2 * 128)
                xr = AR.alloc(256, F32)
                xi = AR.alloc(256, F32)
                yr = AR.alloc(256, F32)
                yi = AR.alloc(256, F32)
                tmp = AR.alloc(256, F32)
                P.dma('sp', Braw_re.v(), IN[ph + '_Bre'].f(lambda a, pr=pr: a[:, pr * 128:(pr + 1) * 128]), ph + 'bre')
                P.dma('sp', Braw_im.v(), IN[ph + '_Bim'].f(lambda a, pr=pr: a[:, pr * 128:(pr + 1) * 128]), ph + 'bim')
                crv, civ = cre.v(0, 128, pr, pr + 1), cim.v(0, 128, pr, pr + 1)
                P.ts(tmpm.v(), Braw_im.v(), civ, None, ALU.mult)
                P.stt(Bb_re.v(), Braw_re.v(), crv, tmpm.v(), ALU.mult, ALU.subtract)
                P.ts(tmpm.v(), Braw_im.v(), crv, None, ALU.mult)
                P.stt(Bb_im.v(), Braw_re.v(), civ, tmpm.v(), ALU.mult, ALU.add)
                if out is not None:
                    Cre = AR.alloc(32, F32)
                    Cim = AR.alloc(32, F32)
                    Cimn = AR.alloc(32, F32)
                    tmc = AR.alloc(32, F32)
                    P.dma('sp', Cre.v(), IN[ph + '_Cre'].f(lambda a, pr=pr: a[:, pr * 32:(pr + 1) * 32]), ph + 'cre')
                    P.dma('sp', Cim.v(), IN[ph + '_Cim'].f(lambda a, pr=pr: a[:, pr * 32:(pr + 1) * 32]), ph + 'cim')
                    P.ts(Cimn.v(), Cim.v(), -1.0, None, ALU.mult)
                for tau in range(8):
                    j = 7 - tau
                    prv, piv = pw.sub(j * 16, 8).v(0, 128, pr, pr + 1), pw.sub(j * 16 + 8, 8).v(0, 128, pr, pr + 1)
                    P.ts(tmpm.v(), Bb_im.v(), piv, None, ALU.mult)
                    P.stt(nat_re.v(), Bb_re.v(), prv, tmpm.v(), ALU.mult, ALU.subtract)
                    P.ts(tmpm.v(), Bb_im.v(), prv, None, ALU.mult)
                    P.stt(nat_im.v(), Bb_re.v(), piv, tmpm.v(), ALU.mult, ALU.add)
                    for part, nat in ((0, nat_re), (1, nat_im)):
                        pb = ps[tk % 4]
                        tk += 1
                        P.tr(pb.v(0, 128, 0, 128), nat.v(), ident_f.v())
                        o = (tau * 2 + part) * 128
                        P.copy(Lin.v(0, 128, o, o + 128), pb.v(0, 128, 0, 128), eng='act')
                    if out is not None:
                        pb = ps[4 + (tk % 2)]
                        ov = pb.v(0, 128, 0, 32)
                        P.mm(ov, nat_re.v(), Cre.v(), True, False)
                        P.mm(ov, nat_im.v(), Cimn.v(), False, True)
                        o = j * 128 + pl * 32
                        P.copy(out['Lk'].v(0, 128, o, o + 32), ov, eng='act')
                        jj = tau + 1
                        p2r = pw.sub(jj * 16, 8).v(0, 128, pr, pr + 1)
                        p2i = pw.sub(jj * 16 + 8, 8).v(0, 128, pr, pr + 1)
                        q2 = pl % 2
                        o_re = ((pl * 8 + tau) * 2 + 0) * 64 + q2 * 32
                        o_im = ((pl * 8 + tau) * 2 + 1) * 64 + q2 * 32
                        H = out['H']
                        P.ts(tmc.v(), Cim.v(), p2i, None, ALU.mult)
                        P.stt(H.v(0, 128, o_re, o_re + 32), Cre.v(), p2r, tmc.v(), ALU.mult, ALU.subtract)
                        P.ts(tmc.v(), Cim.v(), p2r, -1.0, ALU.mult, ALU.mult)
                        P.stt(H.v(0, 128, o_im, o_im + 32), Cre.v(), p2i, tmc.v(), ALU.mult, ALU.subtract)
                        P.ts(H.v(0, 128, o_im, o_im + 32), H.v(0, 128, o_im, o_im + 32), -1.0, None, ALU.mult)
                for part, xx in ((0, xr), (1, xi)):
                    pb = ps[6 + part]
                    for tau in range(8):
                        o = (tau * 2 + part) * 128
                        rhs = zc.v(0, 128, half * TOK, (half + 1) * TOK).f(
                            lambda a, tau=tau: a.rearrange("p (c t) -> p c t", t=8)[:, :, tau])
                        P.mm(pb.v(0, 128, 0, 256), Lin.v(0, 128, o, o + 128), rhs, tau == 0, tau == 7)
                    P.copy(xx.v(), pb.v(0, 128, 0, 256), eng='act')
                if Ire is not None:
                    a_re, a_im = pw.sub(8 * 16, 8).v(0, 128, pr, pr + 1), pw.sub(8 * 16 + 8, 8).v(0, 128, pr, pr + 1)
                    i_re, i_im = Ire.v(0, 128, pr, pr + 1), Iim.v(0, 128, pr, pr + 1)
                    x0r, x0i, t0_ = xr.v(0, 128, 0, 1), xi.v(0, 128, 0, 1), tmp.v(0, 128, 0, 1)
                    P.stt(x0r, i_re, a_re, x0r, ALU.mult, ALU.add)
                    P.ts(t0_, i_im, a_im, None, ALU.mult)
                    P.tt(x0r, x0r, t0_, ALU.subtract)
                    P.stt(x0i, i_re, a_im, x0i, ALU.mult, ALU.add)
                    P.stt(x0i, i_im, a_re, x0i, ALU.mult, ALU.add)
                src_r, src_i, dst_r, dst_i = xr, xi, yr, yi
                for k in range(8):
                    d = 1 << k
                    a_re = pw.sub((8 + k) * 16, 8).v(0, 128, pr, pr + 1)
                    a_im = pw.sub((8 + k) * 16 + 8, 8).v(0, 128, pr, pr + 1)
                    n = 256 - d
                    P.copy(dst_r.v(0, 128, 0, d), src_r.v(0, 128, 0, d), eng='act')
                    P.copy(dst_i.v(0, 128, 0, d), src_i.v(0, 128, 0, d), eng='act')
                    P.stt(tmp.v(0, 128, 0, n), src_r.v(0, 128, 0, n), a_re, src_r.v(0, 128, d, 256), ALU.mult, ALU.add)
                    P.ts(dst_r.v(0, 128, d, 256), src_i.v(0, 128, 0, n), a_im, None, ALU.mult)
                    P.tt(dst_r.v(0, 128, d, 256), tmp.v(0, 128, 0, n), dst_r.v(0, 128, d, 256), ALU.subtract)
                    P.stt(tmp.v(0, 128, 0, n), src_r.v(0, 128, 0, n), a_im, src_i.v(0, 128, d, 256), ALU.mult, ALU.add)
                    P.stt(dst_i.v(0, 128, d, 256), src_i.v(0, 128, 0, n), a_re, tmp.v(0, 128, 0, n), ALU.mult, ALU.add)
                    src_r, dst_r = dst_r, src_r
                    src_i, dst_i = dst_i, src_i
                assert src_r is xr
                if out is None:
                    Fo = SC['Fo']
                    P.copy(Fo.v(0, 128, pr, pr + 1), xr.v(0, 128, 255, 256), eng='act')
                    P.copy(Fo.v(0, 128, 8 + pr, 8 + pr + 1), xi.v(0, 128, 255, 256), eng='act')
                else:
                    Xp = out['Xp']
                    for part, xx, Iv in ((0, xr, Ire), (1, xi, Iim)):
                        o = (pl * 2 + part) * 256
                        P.copy(Xp.v(0, 128, o + 1, o + 256), xx.v(0, 128, 0, 255), eng='act')
                        P.copy(Xp.v(0, 128, o, o + 1), Iv.v(0, 128, pr, pr + 1), eng='act')
                AR.off = m2

        def qk_process(out_v, src_ps, gview, rope, r0, n, scr):
            qg, sq, rs = scr
            RW = rope.ncols // 2
            P.act(sq.v(0, 128, 0, n), src_ps, AF.Square)
            pb2 = ps[6]
            P.mm(pb2.v(0, 128, 0, n), bones_b.v(), sq.v(0, 128, 0, n), True, True)
            P.act(rs.v(0, 128, 0, n), pb2.v(0, 128, 0, n), AF.Ln, scale=1.0 / 64, bias=epsb.v())
            P.act(rs.v(0, 128, 0, n), rs.v(0, 128, 0, n), AF.Exp, scale=-0.5)
            P.stt(qg.v(0, 128, 0, n), src_ps, gview, rs.v(0, 128, 0, n), ALU.mult, ALU.mult)
            pb3 = ps[7]
            P.mm(pb3.v(0, 128, 0, n), prot_b.v(), qg.v(0, 128, 0, n), True, True)
            P.tt(sq.v(0, 128, 0, n), qg.v(0, 128, 0, n), rope.v(0, 128, r0, r0 + n), ALU.mult)
            P.tt(qg.v(0, 128, 0, n), pb3.v(0, 128, 0, n), rope.v(0, 128, RW + r0, RW + r0 + n), ALU.mult)
            P.tt(out_v, sq.v(0, 128, 0, n), qg.v(0, 128, 0, n), ALU.add)

        def lin_fm6(wname, nk, ncols_total, rhs_fn, evac_fn, **kw):
            mi = 0
            cw = WR_SZ // nk
            ntiles = kw.get('ntiles', NT)
            tile_w = kw.get('tile_w', 512)
            for c0 in range(0, ncols_total, cw):
                ncols = min(cw, ncols_total - c0)
                wb = load_w(wname, 0, nk, c0, ncols)
                for ml in range(ncols // 128):
                    m = c0 // 128 + ml
                    for t in range(ntiles):
                        bank = ps[mi % 6]
                        mi += 1
                        for k in range(nk):
                            lv = wb.v(0, 128, k * ncols + ml * 128, k * ncols + (ml + 1) * 128)
                            P.mm(bank.v(0, 128, 0, tile_w), lv, rhs_fn(k, t), k == 0, k == nk - 1)
                        evac_fn(m, t, bank.v(0, 128, 0, tile_w))

        import os as _os
        STOP = 'h1' if mode == 'mix' else _os.environ.get('K_STOP', '')
        SKIP_MIX = (mode == 'ffn') or bool(_os.environ.get('K_SKIP_MIX'))

        def phaseB():
                esink = SM.alloc(8)
                P.act(esink.v(), SM_g['b_sinks'].v(), AF.Exp)
                nb0 = SM.alloc(1)
                P.ts(nb0.v(), SM_g['b_flag'].v(), -1.0, 30000.0, ALU.add, ALU.mult)
                rmsnorm_feat(SM_g['b_g1'])
                dump('xn1', xn.v(0, 128, 0, 512))
                if STOP == 'xn1':
                    return
                def add_to_h(m, t, pv):
                    P.tt(hv(m, t * 512, (t + 1) * 512), hv(m, t * 512, (t + 1) * 512), pv, ALU.add)

                def mixers():
                    yT = AR.alloc(8 * TOK)
                    mixer_mark = AR.off

                    zc = AR.alloc(2 * TOK)
                    lin_fm('b_win_zc', 8, 256, lambda k, t: xv(k, t * 512, (t + 1) * 512),
                           lambda m, t, pv: P.copy(zc.v(0, 128, m * TOK + t * 512, m * TOK + (t + 1) * 512), pv, eng='act'))
                    if STOP == 'zc':
                        return True
                    SC = ssm_scalars('b')
                    pw = SC['pw']
                    Ire, Iim = SM.alloc(8), SM.alloc(8)
                    a_re, a_im = pw.sub(16 * 16, 8), pw.sub(16 * 16 + 8, 8)
                    Gre, Gim = SM_g['b_Gre'], SM_g['b_Gim']
                    tA, tB = SM.alloc(8), SM.alloc(8)
                    P.copy(Ire.v(), Gre.v(0, 128, 0, 8))
                    P.copy(Iim.v(), Gim.v(0, 128, 0, 8))
                    for s in (1, 2):
                        P.tt(tA.v(), Ire.v(), a_re.v(), ALU.mult)
                        P.tt(tB.v(), Iim.v(), a_im.v(), ALU.mult)
                        P.tt(tA.v(), tA.v(), tB.v(), ALU.subtract)
                        P.tt(tB.v(), Ire.v(), a_im.v(), ALU.mult)
                        P.tt(Iim.v(), Iim.v(), a_re.v(), ALU.mult)
                        P.tt(Iim.v(), Iim.v(), tB.v(), ALU.add)
                        P.tt(Iim.v(), Iim.v(), Gim.v(0, 128, s * 8, s * 8 + 8), ALU.add)
                        P.tt(Ire.v(), tA.v(), Gre.v(0, 128, s * 8, s * 8 + 8), ALU.add)
                    yg = AR.alloc(2 * TOK)
                    half_mark = AR.off
                    dsk = SM_g['b_dskip']
                    kk = 0
                    for half in range(2):
                        out = dict(Xp=AR.alloc(4 * 2 * 256), H=AR.alloc(4 * 8 * 2 * 64), Lk=AR.alloc(8 * 128))
                        P.memset(out['H'].v(), 0.0)
                        ssm_half('b', SC, half, zc, Ire, Iim, out)
                        if STOP == 'st':
                            return True
                        Xp, H, Lk = out['Xp'], out['H'], out['Lk']
                        ysc = [AR.alloc(512, F32) for _ in range(2)]
                        for t in range(NT):
                            pb = ps[kk % 4]
                            kk += 1
                            zt = zc.v(0, 128, half * TOK + t * 512, half * TOK + (t + 1) * 512)
                            for lag in range(8):
                                outv = pb.v().f(lambda a, lag=lag: a.rearrange("p (c t) -> p c t", t=8)[:, :, lag:8])
                                rhs = zt.f(lambda a, lag=lag: a.rearrange("p (c t) -> p c t", t=8)[:, :, 0:8 - lag])
                                P.mm(outv, Lk.v(0, 128, lag * 128, (lag + 1) * 128), rhs, lag == 0, False)
                            for tau in range(8):
                                for pl in range(4):
                                    qd = pl // 2
                                    for part in range(2):
                                        o = ((pl * 8 + tau) * 2 + part) * 64
                                        xo = (pl * 2 + part) * 256 + t * 64
                                        outv = pb.v(qd * 64, qd * 64 + 64).f(
                                            lambda a, tau=tau: a.rearrange("p (c t) -> p c t", t=8)[:, :, tau])
                                        last = (tau == 7 and pl % 2 == 1 and part == 1)
                                        P.mm(outv, H.v(0, 128, o, o + 64), Xp.v(0, 128, xo, xo + 64), False, last)
                            ys = ysc[kk % 2]
                            P.stt(ys.v(), zt, dsk.v(0, 128, half, half + 1), pb.v(), ALU.mult, ALU.add)
                            P.act(yg.v(0, 128, half * TOK + t * 512, half * TOK + (t + 1) * 512), ys.v(), AF.Gelu)
                        AR.off = half_mark
                    dump('yg', yg.v(0, 128, 0, 512))
                    if STOP == 'yg':
                        return True
                    gl1 = AR.alloc(2 * 256)
                    gl2 = AR.alloc(2 * 256)
                    for nm, gb in (('b_glu1', gl1), ('b_glu2', gl2)):
                        P.dma('pool', gb.v().f(lambda a: a.rearrange("p (k n) -> p k n", k=2)),
                              IN[nm].f(lambda a: a.rearrange("(k p) n -> p k n", p=128)), nm)
                    if STOP == 'g1':
                        return True
                    ycf = AR.alloc(2 * 512, F32)
                    sgs = AR.alloc(512, F32)
                    ysq = AR.alloc(512)
                    for t in range(NT):
                        t0, t1 = t * 512, (t + 1) * 512
                        for m in range(2):
                            p1_, p2_ = ps[4 + m * 2], ps[5 + m * 2]
                            for k in range(2):
                                rhs = yg.v(0, 128, k * TOK + t0, k * TOK + t1)
                                P.mm(p1_.v(), gl1.v(0, 128, k * 256 + m * 128, k * 256 + (m + 1) * 128), rhs, k == 0, k == 1)
                            for k in range(2):
                                rhs = yg.v(0, 128, k * TOK + t0, k * TOK + t1)
                                P.mm(p2_.v(), gl2.v(0, 128, k * 256 + m * 128, k * 256 + (m + 1) * 128), rhs, k == 0, k == 1)
                            P.act(sgs.v(), p2_.v(), AF.Sigmoid)
                            P.tt(ycf.v(0, 128, m * 512, (m + 1) * 512), p1_.v(), sgs.v(), ALU.mult)
                        if STOP == 'g2':
                            return True
                        pbs = ps[(t % 2)]
                        for m in range(2):
                            P.act(ysq.v(), ycf.v(0, 128, m * 512, (m + 1) * 512), AF.Square)
                            P.mm(pbs.v(), ones_b.v(), ysq.v(), m == 0, m == 1)
                        P.act(ln_s.v(), pbs.v(), AF.Ln, scale=1.0 / 256, bias=epsb.v())
                        P.act(rstd_s.v(), ln_s.v(), AF.Exp, scale=-0.5)
                        for m in range(2):
                            P.stt(yT.v(0, 128, (6 + m) * TOK + t0, (6 + m) * TOK + t1), ycf.v(0, 128, m * 512, (m + 1) * 512),
                                  SM_g['b_gmix'].v(0, 128, 6 + m, 7 + m), rstd_s.v(), ALU.mult, ALU.mult)
                    dump('yc', yT.v(0, 128, 6 * TOK, 6 * TOK + 512))
                    if STOP == 'yc':
                        return True
                    AR.off = mixer_mark

                    zag = AR.alloc(16 * 512)
                    wsT = AR.alloc(512)
                    lngb = AR.alloc(512)
                    P.dma('pool', wsT.v(), IN['b_wsT'], 'wsT')
                    for h in range(4):
                        P.tt(wsT.v(0, 128, h * 128, (h + 1) * 128), wsT.v(0, 128, h * 128, (h + 1) * 128), mtri_b.v(), ALU.mult)
                    P.dma('pool', lngb.v(), IN['b_lngb'], 'lngb')
                    for grp in range(4):
                        for kq in range(4):
                            wb = load_w('b_win_za', kq * 256, 2, 0, 512)
                            for bi in range(4):
                                blk = grp * 4 + bi
                                pb = ps[bi]
                                for k2 in range(2):
                                    k = kq * 2 + k2
                                    P.mm(pb.v(), xv(k, blk * 128, (blk + 1) * 128), wb.v(0, 128, k2 * 512, (k2 + 1) * 512),
                                         k == 0, k == 7)
                        for bi in range(4):
                            blk = grp * 4 + bi
                            P.act(zag.v(0, 128, blk * 512, (blk + 1) * 512), ps[bi].v(), AF.Gelu)
                    vsel = lambda a: a.rearrange("p (c h x) -> p c h x", c=16, h=4, x=128)[:, :, :, 64:128]
                    usel = lambda a: a.rearrange("p (c h x) -> p c h x", c=16, h=4, x=128)[:, :, :, 0:64]
                    vn = AR.alloc(16 * 256)
                    vn4 = lambda a: a.rearrange("p (c h d) -> p c h d", c=16, h=4, d=64)
                    st1 = SM.alloc(64)
                    st2 = SM.alloc(64)
                    st3 = SM.alloc(64)
                    bc64 = lambda a: a.rearrange("p (c h) -> p c h", c=16).unsqueeze(3).to_broadcast([128, 16, 4, 64])
                    st3d = lambda a: a.rearrange("p (c h) -> p c h", c=16)
                    P.red(st1.v().f(st3d), zag.v().f(vsel))
                    P.tt(vn.v().f(vn4), zag.v().f(vsel), zag.v().f(vsel), ALU.mult)
                    P.red(st2.v().f(st3d), vn.v().f(vn4))
                    P.ts(st1.v(), st1.v(), 1.0 / 64, None, ALU.mult)
                    P.tt(st3.v(), st1.v(), st1.v(), ALU.mult)
                    P.stt(st2.v(), st2.v(), 1.0 / 64, st3.v(), ALU.mult, ALU.subtract)
                    P.act(st2.v(), st2.v(), AF.Sqrt, bias=epsb.v())
                    P.recip(st2.v(), st2.v())
                    P.tt(vn.v().f(vn4), zag.v().f(vsel), st1.v().f(bc64), ALU.subtract)
                    P.tt(vn.v().f(vn4), vn.v().f(vn4), st2.v().f(bc64), ALU.mult)
                    gsel = lambda a: a.rearrange("p (h d) -> p h d", h=4).unsqueeze(1).to_broadcast([128, 16, 4, 64])
                    P.tt(vn.v().f(vn4), vn.v().f(vn4), lngb.v(0, 128, 0, 256).f(gsel), ALU.mult)
                    P.tt(vn.v().f(vn4), vn.v().f(vn4), lngb.v(0, 128, 256, 512).f(gsel), ALU.add)
                    ya = AR.alloc(16 * 256)
                    ya4 = lambda a: a.rearrange("p (c h d) -> p c h d", c=16, h=4, d=64)
                    for h in range(4):
                        for cc in range(2):
                            pb = ps[4 + (h * 2 + cc) % 4]
                            rhs = vn.v().f(lambda a, h=h, cc=cc: vn4(a)[:, cc * 8:(cc + 1) * 8, h, :])
                            outv = pb.v().f(lambda a: a.rearrange("p (c d) -> p c d", d=64))
                            P.mm(outv, wsT.v(0, 128, h * 128, (h + 1) * 128), rhs, True, True)
                            uv = zag.v().f(lambda a, h=h, cc=cc: usel(a)[:, cc * 8:(cc + 1) * 8, h, :])
                            ov = ya.v().f(lambda a, h=h, cc=cc: ya4(a)[:, cc * 8:(cc + 1) * 8, h, :])
                            P.stt(ov, outv, SM_g['b_bs'].v(0, 128, h, h + 1), uv, ALU.add, ALU.mult)
                    yasq = AR.alloc(16 * 256)
                    sa = SM.alloc(16)
                    y3 = lambda a: a.rearrange("p (c f) -> p c f", c=16)
                    P.tt(yasq.v().f(y3), ya.v().f(y3), ya.v().f(y3), ALU.mult)
                    P.red(sa.v(), yasq.v().f(y3))
                    P.act(sa.v(), sa.v(), AF.Sqrt, scale=1.0 / 256, bias=epsb.v())
                    P.recip(sa.v(), sa.v())
                    P.tt(yasq.v().f(y3), ya.v().f(y3), sa.v().f(lambda a: a.unsqueeze(2).to_broadcast([128, 16, 256])), ALU.mult)
                    k = 0
                    for blk in range(16):
                        for m in range(2):
                            pb = ps[k % 4]
                            k += 1
                            pv = pb.v(0, 128, 0, 64).f(lambda a: a.bitcast(BF16))
                            P.tr(pv, yasq.v(0, 128, blk * 256 + m * 128, blk * 256 + (m + 1) * 128), ident_b.v())
                            P.act(yT.v(0, 128, m * TOK + blk * 128, m * TOK + (blk + 1) * 128), pv, AF.Copy,
                                  scale=SM_g['b_gmix'].v(0, 128, m, m + 1))
                    dump('ya', yT.v(0, 128, 0, 512))
                    if STOP == 'ya':
                        return True
                    AR.off = mixer_mark

                    rope = AR.alloc(2 * TOK)
                    make_rope(rope, 0, TOK)
                    qf = AR.alloc(4 * TOK)
                    KW = TOK + 128
                    kf = AR.alloc(2 * KW)
                    vt = AR.alloc(17 * 260)
                    scr = (AR.alloc(512), AR.alloc(512), AR.alloc(512, F32))
                    for kh in range(2):
                        P.copy(kf.v(0, 128, kh * KW, kh * KW + 128), SM_g['b_khalo'].v(0, 128, kh * 128, (kh + 1) * 128))
                    P.memset(vt.v(), 1.0)
                    v4 = lambda a: a.rearrange("p (b k x) -> p b k x", b=17, k=2, x=130)
                    P.copy(vt.v().f(lambda a: v4(a)[:, 0, :, 0:64]),
                           SM_g['b_vhalo'].v().f(lambda a: a.rearrange("p (k d) -> p k d", k=2)))
                    if STOP == 'b1':
                        return True
                    lin_fm6('b_win_q', 8, 512, lambda k, t: xv(k, t * 512, (t + 1) * 512),
                            lambda m, t, pv: qk_process(qf.v(0, 128, m * TOK + t * 512, m * TOK + (t + 1) * 512), pv,
                                                        SM_g['b_qg'].v(), rope, t * 512, 512, scr))
                    lin_fm6('b_win_k2', 8, 256, lambda k, t: xv(k, t * 512, (t + 1) * 512),
                            lambda m, t, pv: qk_process(kf.v(0, 128, m * KW + 128 + t * 512, m * KW + 128 + (t + 1) * 512), pv,
                                                        SM_g['b_kg'].v(), rope, t * 512, 512, scr))
                    wv = load_w('b_win_v', 0, 8, 0, 128)
                    for blk in range(16):
                        pb = ps[blk % 4]
                        for k in range(8):
                            P.mm(pb.v(0, 128, 0, 128), xv(k, blk * 128, (blk + 1) * 128), wv.v(0, 128, k * 128, (k + 1) * 128),
                                 k == 0, k == 7)
                        P.copy(vt.v().f(lambda a, blk=blk: v4(a)[:, blk + 1, :, 0:64]),
                               pb.v(0, 128, 0, 128).f(lambda a: a.rearrange("p (k d) -> p k d", k=2)), eng='act')
                    dump('qf', qf.v(0, 128, 0, 512))
                    dump('kf', kf.v(0, 128, 128, 640))
                    if STOP == 'b2':
                        return True
                    xoff = [0]

                    def xalloc(n, dtype=BF16):
                        b = Buf('xn', xn.t, n, dtype, xoff[0], BF16)
                        xoff[0] += n * DSZ[dtype]
                        return b
                    pexp = [xalloc(512) for _ in range(4)]
                    ybt = [xalloc(512, F32) for _ in range(2)]
                    ybn = [xalloc(512) for _ in range(2)]
                    ysq2 = xalloc(512)
                    dn = SM.alloc(16)
                    ssq = SM.alloc(4)
                    pk = 0
                    for blk in range(16):
                        yb = ybt[blk % 2]
                        for kh in range(2):
                            po = ps[4 + (blk * 2 + kh) % 2]
                            pes = []
                            for kb in range(2):
                                pSe, pSo = ps[kb * 2], ps[kb * 2 + 1]
                                pe_ = pexp[pk % 4]
                                pk += 1
                                pes.append(pe_)
                                kcol = kh * KW + (blk + kb) * 128
                                for hh in range(4):
                                    c = kh * 2 + hh // 2
                                    p0 = (hh % 2) * 64
                                    pS = pSe if hh % 2 == 0 else pSo
                                    j2 = hh // 2
                                    P.mm(pS.v(0, 128, j2 * 128, (j2 + 1) * 128), kf.v(p0, p0 + 64, kcol, kcol + 128),
                                         qf.v(p0, p0 + 64, c * TOK + blk * 128, c * TOK + (blk + 1) * 128), True, True)
                                for par, pS in ((0, pSe), (1, pSo)):
                                    ov = pe_.v().f(lambda a, par=par: a.rearrange("p (a b q) -> p a b q", a=2, b=2)[:, :, par, :])
                                    iv = pS.v(0, 128, 0, 256).f(lambda a: a.rearrange("p (a q) -> p a q", a=2))
                                    if kb == 0 and blk == 0:
                                        P.act(ov, iv, AF.Exp, scale=0.125, bias=nb0.v())
                                    else:
                                        P.act(ov, iv, AF.Exp, scale=0.125)
                                msk = (mprev_b if kb == 0 else mcur_b).v().f(lambda a: a.unsqueeze(1).to_broadcast([128, 4, 128]))
                                P.tt(pe_.v().f(lambda a: a.rearrange("p (h q) -> p h q", h=4)),
                                     pe_.v().f(lambda a: a.rearrange("p (h q) -> p h q", h=4)), msk, ALU.mult)
                            for hh in range(4):
                                for kb in range(2):
                                    rhs = vt.v().f(lambda a, blk=blk, kb=kb, kh=kh: v4(a)[:, blk + kb, kh, 0:65])
                                    P.mm(po.v(0, 128, hh * 65, (hh + 1) * 65), pes[kb].v(0, 128, hh * 128, (hh + 1) * 128), rhs,
                                         kb == 0, kb == 1)
                            o4 = lambda a: a.rearrange("p (h x) -> p h x", x=65)
                            dv = dn.v(0, 128, kh * 4, kh * 4 + 4)
                            P.tt(dv.f(lambda a: a.unsqueeze(2)), po.v(0, 128, 0, 260).f(lambda a: o4(a)[:, :, 64:65]),
                                 esink.v(0, 128, kh * 4, kh * 4 + 4).f(lambda a: a.unsqueeze(2)), ALU.add)
                            P.recip(dv, dv)
                            P.tt(yb.v(0, 128, kh * 256, (kh + 1) * 256).f(lambda a: a.rearrange("p (h d) -> p h d", h=4)),
                                 po.v(0, 128, 0, 260).f(lambda a: o4(a)[:, :, 0:64]),
                                 dv.f(lambda a: a.unsqueeze(2).to_broadcast([128, 4, 64])), ALU.mult)
                        if STOP == 'b3':
                            return True
                        sv = ssq.v(0, 128, blk % 4, blk % 4 + 1)
                        P.tt(ysq2.v(), yb.v(), yb.v(), ALU.mult)
                        P.red(sv, ysq2.v())
                        P.act(sv, sv, AF.Sqrt, scale=1.0 / 512, bias=epsb.v())
                        P.recip(sv, sv)
                        yn = ybn[blk % 2]
                        P.ts(yn.v(), yb.v(), sv, None, ALU.mult)
                        for m in range(4):
                            pb = ps[6 + (blk * 4 + m) % 2]
                            pv = pb.v(0, 128, 0, 64).f(lambda a: a.bitcast(BF16))
                            P.tr(pv, yn.v(0, 128, m * 128, (m + 1) * 128), ident_b.v())
                            P.act(yT.v(0, 128, (2 + m) * TOK + blk * 128, (2 + m) * TOK + (blk + 1) * 128), pv, AF.Copy,
                                  scale=SM_g['b_gmix'].v(0, 128, 2 + m, 3 + m))
                    dump('yb', yT.v(0, 128, 2 * TOK, 2 * TOK + 512))
                    if STOP == 'yb':
                        return True
                    AR.off = mixer_mark

                    lin_fm('b_wout', 8, 1024, lambda k, t: yT.v(0, 128, k * TOK + t * 512, k * TOK + (t + 1) * 512), add_to_h)
                    dump('h1', hb.v(0, 128, 0, 512))
                    if STOP == 'h1':
                        return True
                    AR.off = base_mark


                if not SKIP_MIX:
                    if mixers():
                        return
                AR.off = base_mark
                rmsnorm_feat(SM_g['b_g2'])
                hid = [AR.alloc(4 * TOK) for _ in range(2)]
                rl = [AR.alloc(512, F32) for _ in range(3)]
                rk = [0]
                for g in range(int(_os.environ.get('K_FFN_G', '8'))):
                    hd = hid[g % 2]

                    def ev1(m, t, pv, hd=hd):
                        r = rl[rk[0] % 3]
                        rk[0] += 1
                        P.act(r.v(), pv, AF.Relu)
                        P.tt(hd.v(0, 128, m * TOK + t * 512, m * TOK + (t + 1) * 512), r.v(), r.v(), ALU.mult)
                    lin_fm('b_wff1', 8, 512, lambda k, t: xv(k, t * 512, (t + 1) * 512), ev1, c_base=g * 512)
                    lin_fm('b_wff2', 4, 1024, lambda k, t, hd=hd: hd.v(0, 128, k * TOK + t * 512, k * TOK + (t + 1) * 512),
                           add_to_h, r0=g * 512)
                dump('h2', hb.v(0, 128, 0, 512))
                if STOP == 'h2':
                    return
                AR.off = base_mark

                rmsnorm_feat(SM_g['b_g3'])
                pT = AR.alloc(2 * TOK)
                P.dma('pool', pT.v().f(lambda a: a.rearrange("p (k n) -> p k n", k=2)),
                      IN['b_pT'].f(lambda a: a.rearrange("(k p) n -> p k n", p=128)), 'pT')
                wp = AR.alloc(2 * 1024)
                P.dma('pool', wp.v().f(lambda a: a.rearrange("p (k n) -> p k n", k=2)),
                      IN['b_wp'].f(lambda a: a.rearrange("(k p) n -> p k n", p=128)), 'wp')
                gs = [AR.alloc(512, F32) for _ in range(2)]
                gk = [0]

                def ev_ple(m, t, pv):
                    g_ = gs[gk[0] % 2]
                    gk[0] += 1
                    P.act(g_.v(), pv, AF.Sigmoid)
                    for k in range(2):
                        P.mm(pv, wp.v(0, 128, k * 1024 + m * 128, k * 1024 + (m + 1) * 128),
                             pT.v(0, 128, k * TOK + t * 512, k * TOK + (t + 1) * 512), k == 0, k == 1)
                    P.tt(g_.v(), pv, g_.v(), ALU.mult)
                    P.tt(hv(m, t * 512, (t + 1) * 512), hv(m, t * 512, (t + 1) * 512), g_.v(), ALU.add)
                lin_fm('b_wg', 8, 1024, lambda k, t: xv(k, t * 512, (t + 1) * 512), ev_ple)
        if hasB:
            phaseB()
            AR.off = base_mark
            for c in range(8):
                P.dma('sp', hT_out.f(lambda a, c=c: a[c * 128:(c + 1) * 128, :]), hb.v(0, 128, c * TOK, (c + 1) * TOK), 'out')

        if hasA:
            rmsnorm_feat(SM_g['a_g1'])
            zc = AR.alloc(2 * TOK)
            lin_fm('a_win_zc', 8, 256, lambda k, t: xv(k, t * 512, (t + 1) * 512),
                   lambda m, t, pv: P.copy(zc.v(0, 128, m * TOK + t * 512, m * TOK + (t + 1) * 512), pv, eng='act'))
            SC = ssm_scalars('a')
            SC['Fo'] = SM.alloc(16)
            for half in range(2):
                ssm_half('a', SC, half, zc, None, None, None)
            P.dma('sp', oF, SC['Fo'].v(), 'out')
            rope = AR.alloc(2 * 128)
            t0 = TOK - 128
            make_rope(rope, t0, TOK)
            scr = (AR.alloc(512), AR.alloc(512), AR.alloc(512, F32))
            kho = AR.alloc(256)
            khf = AR.alloc(256, F32)
            lin_fm6('a_win_k2', 8, 256, lambda k, t: xv(k, t0, TOK),
                    lambda m, t, pv: qk_process(kho.v(0, 128, m * 128, (m + 1) * 128), pv, SM_g['a_kg'].v(), rope, 0, 128, scr),
                    ntiles=1, tile_w=128)
            P.copy(khf.v(), kho.v())
            P.dma('sp', okh, khf.v(), 'out')
            wv = load_w('a_win_v', 0, 8, 0, 128)
            pb = ps[0]
            for k in range(8):
                P.mm(pb.v(0, 128, 0, 128), xv(k, t0, TOK), wv.v(0, 128, k * 128, (k + 1) * 128), k == 0, k == 7)
            vho = AR.alloc(128, F32)
            P.copy(vho.v(), pb.v(0, 128, 0, 128), eng='act')
            P.dma('sp', ovh, vho.v(), 'out')
        fin = ['out'] + (['dbg'] if dbg_out else [])
        print("arena peak KB", AR.peak / 1024, "small", SM.off, "ops", {e: len(P.ops[e]) for e in ENGS})
        P.emit(final_waits=fin)
    return nc


def _consts():
    p = np.arange(128)
    inv = (1.0 / (10000.0 ** (np.arange(0, 64, 2, dtype=np.float32) / 64))).astype(np.float32)
    tab = np.zeros((128, 16), np.float32)
    tab[:, 0] = inv[p % 32]
    ident = np.eye(128, dtype=np.float32)
    ones = np.ones((128, 128), np.float32)
    bones = np.zeros((128, 128), np.float32)
    bones[:64, :64] = 1
    bones[64:, 64:] = 1
    prot = np.zeros((128, 128), np.float32)
    for m in range(128):
        if (m % 64) < 32:
            prot[m + 32, m] = -1.0
        else:
            prot[m - 32, m] = 1.0
    tri = (p[:, None] <= p[None, :]).astype(np.float32)
    mcur = tri.copy()
    mprev = (p[:, None] > p[None, :]).astype(np.float32)
    mats = np.concatenate([ident, ones, bones, prot, tri, mcur, mprev], axis=1)
    return tab, np.ascontiguousarray(mats)


def _chunkcols(v):
    return np.ascontiguousarray(v.reshape(8, 128).T)


def _layer_inputs(ph, L, W, full):
    d = {}
    d[ph + '_g1'] = _chunkcols(W['attn_norm_g'][L])
    win = W['w_in'][L]
    k = win[:, 1024:1152]
    d[ph + '_win_k2'] = np.ascontiguousarray(np.concatenate([k[:, 0:64], k[:, 0:64], k[:, 64:128], k[:, 64:128]], axis=1))
    d[ph + '_win_v'] = np.ascontiguousarray(win[:, 1152:1280])
    d[ph + '_win_zc'] = np.ascontiguousarray(win[:, 1280:1536])
    d[ph + '_kg'] = np.ascontiguousarray(np.tile(W['k_norm_g'][L], 2).reshape(128, 1))
    def nat(a):
        return np.ascontiguousarray(a.reshape(8, 2, 64).transpose(1, 2, 0).reshape(128, 8))
    ldt = np.repeat(W['ssm_log_dt'][L][:, None], 64, axis=1)
    d[ph + '_ssm3'] = np.concatenate([nat(W['ssm_a_re'][L]), nat(W['ssm_a_im'][L]), nat(ldt)], axis=1)

    def bbig(b):
        out = np.zeros((128, 8, 128), np.float32)
        for pr in range(8):
            for gl in range(2):
                g = pr * 2 + gl
                gh = g % 8
                out[gl * 64:(gl + 1) * 64, pr, gh * 16:(gh + 1) * 16] = b[g]
        return out.reshape(128, 1024)
    d[ph + '_Bre'] = bbig(W['ssm_b_re'][L])
    d[ph + '_Bim'] = bbig(W['ssm_b_im'][L])
    if not full:
        return d
    d['b_g2'] = _chunkcols(W['mlp_norm_g'][L])
    d['b_g3'] = _chunkcols(W['ple_norm_g'][L])
    d['b_gmix'] = _chunkcols(W['mix_out_g'][L])
    d['b_win_za'] = np.ascontiguousarray(win[:, 0:512])
    d['b_win_q'] = np.ascontiguousarray(win[:, 512:1024])
    lng = np.broadcast_to(W['gmlp_ln_g'][L].reshape(1, 256), (128, 256))
    lnb = np.broadcast_to(W['gmlp_ln_b'][L].reshape(1, 256), (128, 256))
    d['b_lngb'] = np.ascontiguousarray(np.concatenate([lng, lnb], axis=1))
    d['b_wsT'] = np.ascontiguousarray(W['gmlp_ws'][L].transpose(2, 0, 1).reshape(128, 512))
    d['b_bs'] = np.ascontiguousarray(W['gmlp_bs'][L].T)
    d['b_qg'] = np.ascontiguousarray(np.tile(W['q_norm_g'][L], 2).reshape(128, 1))
    d['b_sinks'] = np.ascontiguousarray(np.broadcast_to(W['sinks'][L].reshape(1, 8), (128, 8)))

    def cnat(c):
        out = np.zeros((128, 8, 32), np.float32)
        for pr in range(8):
            for gl in range(2):
                g = pr * 2 + gl
                out[gl * 64:(gl + 1) * 64, pr, gl * 16:(gl + 1) * 16] = c[g].T
        return out.reshape(128, 256)
    d['b_Cre'] = cnat(W['ssm_c_re'][L])
    d['b_Cim'] = cnat(W['ssm_c_im'][L])
    d['b_dskip'] = np.ascontiguousarray(W['ssm_d'][L].reshape(2, 128).T)
    for nm, key in (('b_glu1', 'glu_w1'), ('b_glu2', 'glu_w2'), ('b_wout', 'w_out'), ('b_wff1', 'w_ff1'),
                    ('b_wff2', 'w_ff2'), ('b_wg', 'w_ple_gate'), ('b_wp', 'w_ple_proj')):
        d[nm] = np.ascontiguousarray(W[key][L])
    return d


_PROGS = {}


def _prog(hasB, hasA, mode):
    key = (hasB, hasA, mode)
    if key not in _PROGS:
        _PROGS[key] = build(hasB, hasA, None, mode)
    return _PROGS[key]


def kernel(**inputs):
    W = {k: np.asarray(v) for k, v in inputs.items()}
    x = W['x']
    p = W['p']
    pos = W['positions']
    NC = 8
    tab, mats = _consts()
    hT = [np.ascontiguousarray(x[c // 4, (c % 4) * TOK:(c % 4 + 1) * TOK, :].T) for c in range(NC)]
    posc = [np.ascontiguousarray(pos[c // 4, (c % 4) * TOK:(c % 4 + 1) * TOK].reshape(1, TOK)).astype(np.int32)
            for c in range(NC)]
    zero_g = np.zeros((128, 24), np.float32)
    zero_k = np.zeros((128, 256), np.float32)
    zero_v = np.zeros((128, 128), np.float32)
    zero_f = np.zeros((128, 1), np.float32)
    one_f = np.ones((128, 1), np.float32)

    def launch(hasB, hasA, mode, Lb, La, exch):
        nc = _prog(hasB, hasA, mode)
        shared = {'c_tab': tab, 'c_mats': mats}
        if hasB:
            shared.update(_layer_inputs('b', Lb, W, True))
        if hasA:
            shared.update(_layer_inputs('a', La, W, False))
        in_maps = []
        for c in range(NC):
            m = dict(shared)
            m['hT'] = hT[c]
            m['pos'] = posc[c]
            if hasB:
                b, q = c // 4, c % 4
                m['b_pT'] = np.ascontiguousarray(p[Lb, b, q * TOK:(q + 1) * TOK, :].T)
                gre, gim = zero_g, zero_g
                kh, vh, fl = zero_k, zero_v, zero_f
                if exch is not None:
                    gre = np.zeros((128, 24), np.float32)
                    gim = np.zeros((128, 24), np.float32)
                    for s in range(3):
                        src = q - 3 + s
                        if src >= 0:
                            F = exch[b * 4 + src]['oF']
                            gre[:, s * 8:(s + 1) * 8] = F[:, 0:8]
                            gim[:, s * 8:(s + 1) * 8] = F[:, 8:16]
                    if q > 0:
                        kh, vh, fl = exch[c - 1]['okh'], exch[c - 1]['ovh'], one_f
                m['b_Gre'], m['b_Gim'] = gre, gim
                m['b_khalo'], m['b_vhalo'], m['b_flag'] = kh, vh, fl
            in_maps.append(m)
        return run_bass_kernel_spmd(nc, in_maps, core_ids=list(range(NC)))

    res = launch(False, True, 'all', None, 0, None)
    exch = [{k: np.asarray(res.results[c][k]) for k in ('oF', 'okh', 'ovh')} for c in range(NC)]
    for L in range(4):
        hasA = L < 3
        res = launch(True, hasA, 'all', L, L + 1 if hasA else None, exch)
        hT = [np.asarray(res.results[c]['hT_out']) for c in range(NC)]
        if hasA:
            exch = [{k: np.asarray(res.results[c][k]) for k in ('oF', 'okh', 'ovh')} for c in range(NC)]
    out = np.empty_like(x)
    for c in range(NC):
        out[c // 4, (c % 4) * TOK:(c % 4 + 1) * TOK, :] = hT[c].T
    return out
```
